# Optimizing a Trainium2 kernel written in Bass

```python
import math
import jax, jax.numpy as jnp
from jax import lax
import numpy as np

D_MODEL = 2048
BATCH = 4
SEQ = 2048
DEPTH = 2
DEC_BATCH = 128
DEC_SEQ = 4
PAST_LEN = 2048
PAGE_SIZE = 128

N_BRANCH = 4
D_BRANCH = 3 * D_MODEL // 8
MIX_WIDTH = N_BRANCH * D_BRANCH
A_GROUPS = ((128, 1), (512, 4), (2048, 16))
A_HEADS = 6
A_HPG = A_HEADS // len(A_GROUPS)
A_HEAD_DIM = D_BRANCH // A_HEADS
Q_BLOCK = 128
REL_BUCKETS = 32
REL_MAX_DIST = 2048
B_CHUNK = 128
B_GROUPS = 4
B_GROUP_DIM = D_BRANCH // B_GROUPS
C_WINDOWS = (2, 4, 8, 16)
C_GROUPS = len(C_WINDOWS)
C_GROUP_DIM = D_BRANCH // C_GROUPS
C_PREFIX = max(C_WINDOWS) - 1
MEM_LEN = 256
M_HEADS = 4
M_HEAD_DIM = D_BRANCH // M_HEADS
D_FF = ((8 * D_MODEL // 3 + 127) // 128) * 128
CONV_W = 3
EPS = 1e-6
COL_A = 0
COL_B = COL_A + 3 * D_BRANCH
COL_C = COL_B + 2 * D_BRANCH
COL_M = COL_C + D_BRANCH
COL_G = COL_M + D_BRANCH
IN_COLS = COL_G + N_BRANCH * D_BRANCH

kernel_name = 'hybrid_dilated_gmlp_pool_memory_decoder_step'


def rmsnorm(x, g):
    xf = x.astype(jnp.float32)
    y = xf * lax.rsqrt(jnp.mean(xf * xf, axis=-1, keepdims=True) + EPS)
    return (y * g.astype(jnp.float32)).astype(x.dtype)


def layernorm(x, g):
    xf = x.astype(jnp.float32)
    xc = xf - jnp.mean(xf, axis=-1, keepdims=True)
    y = xc * lax.rsqrt(jnp.mean(xc * xc, axis=-1, keepdims=True) + EPS)
    return (y * g.astype(jnp.float32)).astype(x.dtype)


def rel_bucket(dist):
    exact = REL_BUCKETS // 2
    d = jnp.maximum(dist.astype(jnp.float32), 1.0)
    log_b = exact + (jnp.log(d / exact) / math.log(REL_MAX_DIST / exact) * (REL_BUCKETS - exact)).astype(jnp.int32)
    return jnp.where(dist < exact, dist, jnp.minimum(log_b, REL_BUCKETS - 1))


def dilated_group_attention(q, kv_k, kv_v, q_idx, dist, bias):
    Bn, Nq, H, Dh = q.shape
    qb = Q_BLOCK if Nq % Q_BLOCK == 0 else Nq
    nb = Nq // qb
    q_blocks = q.reshape(Bn, nb, qb, H, Dh).transpose(1, 0, 2, 3, 4)
    idx_blocks = q_idx.reshape(nb, qb)
    scale = Dh ** -0.5
    bias_f = bias.astype(jnp.float32)[None, :, None, :]

    def one_block(args):
        qblk, qi = args
        kidx = qi[:, None] - dist[None, :]
        valid = kidx >= 0
        kidx = jnp.maximum(kidx, 0)
        kg = jnp.take(kv_k, kidx, axis=1)
        vg = jnp.take(kv_v, kidx, axis=1)
        logits = jnp.einsum('bqhd,bqkhd->bhqk', qblk, kg).astype(jnp.float32) * scale + bias_f
        logits = jnp.where(valid[None, None], logits, -1e30)
        lse = jax.nn.logsumexp(logits, axis=-1)
        p = jnp.exp(logits - lse[..., None]).astype(vg.dtype)
        out = jnp.einsum('bhqk,bqkhd->bqhd', p, vg)
        return out, lse

    out, lse = lax.map(one_block, (q_blocks, idx_blocks))
    out = out.transpose(1, 0, 2, 3, 4).reshape(Bn, Nq, H, Dh)
    lse = lse.transpose(1, 2, 0, 3).reshape(Bn, H, Nq)
    return out, lse


def dilated_mixer(qkv, prefixes, rel_bias):
    Bn, N, _ = qkv.shape
    q, k, v = jnp.split(qkv, 3, axis=-1)
    q = q.reshape(Bn, N, A_HEADS, A_HEAD_DIM)
    k = k.reshape(Bn, N, A_HEADS, A_HEAD_DIM)
    v = v.reshape(Bn, N, A_HEADS, A_HEAD_DIM)
    outs, lses, new_rows = [], [], []
    for g, (win, dil) in enumerate(A_GROUPS):
        hs = slice(g * A_HPG, (g + 1) * A_HPG)
        qg, kg, vg = q[:, :, hs], k[:, :, hs], v[:, :, hs]
        pre = prefixes[g]
        kv_k = jnp.concatenate([pre[:, :, 0], kg], axis=1)
        kv_v = jnp.concatenate([pre[:, :, 1], vg], axis=1)
        dist = jnp.arange(win // dil + 1, dtype=jnp.int32) * dil
        bias = rel_bias[rel_bucket(dist)][:, hs].T
        q_idx = pre.shape[1] + jnp.arange(N, dtype=jnp.int32)
        o, lse = dilated_group_attention(qg, kv_k, kv_v, q_idx, dist, bias)
        outs.append(o)
        lses.append(lse)
        keep = min(win, N)
        new_rows.append(jnp.stack([kg, vg], axis=2)[:, N - keep:])
    alpha = jax.nn.softmax(jnp.stack(lses, axis=0), axis=0)
    alpha = alpha.transpose(0, 1, 3, 2)[..., None].astype(qkv.dtype)
    y = jnp.concatenate([outs[g] * alpha[g] for g in range(len(A_GROUPS))], axis=2)
    return y.reshape(Bn, N, D_BRANCH), new_rows


def spatial_gating_mixer(uv, v_gain, w_s, b_s):
    Bn, N, _ = uv.shape
    uv = jax.nn.gelu(uv, approximate=True)
    u, v = jnp.split(uv, 2, axis=-1)
    v = layernorm(v, v_gain)
    pad = (-N) % B_CHUNK
    nc = (N + pad) // B_CHUNK
    vc = jnp.pad(v, ((0, 0), (0, pad), (0, 0))).reshape(Bn, nc, B_CHUNK, B_GROUPS, B_GROUP_DIM)
    w_causal = w_s * jnp.tril(jnp.ones((B_CHUNK, B_CHUNK), w_s.dtype))
    mixed = jnp.einsum('gij,bnjgc->bnigc', w_causal, vc) + b_s.T[None, None, :, :, None]
    mixed = mixed.reshape(Bn, nc * B_CHUNK, D_BRANCH)[:, :N]
    return u * mixed, v


def pooling_mixer(xc, prefix, pos0, w_c, c_scale):
    Bn, N, _ = xc.shape
    ext = jnp.concatenate([prefix, xc], axis=1)
    cs = jnp.pad(jnp.cumsum(ext.astype(jnp.float32), axis=1), ((0, 0), (1, 0), (0, 0)))
    end = cs[:, C_PREFIX + 1:C_PREFIX + 1 + N]
    pos = pos0 + jnp.arange(N, dtype=jnp.int32)
    xf = xc.astype(jnp.float32)
    diffs = []
    for g, w in enumerate(C_WINDOWS):
        cols = slice(g * C_GROUP_DIM, (g + 1) * C_GROUP_DIM)
        s = end[..., cols] - cs[:, C_PREFIX + 1 - w:C_PREFIX + 1 - w + N, cols]
        cnt = jnp.minimum(pos + 1, w).astype(jnp.float32)
        diffs.append(s / cnt[None, :, None] - xf[..., cols])
    d = jnp.stack(diffs, axis=2).astype(xc.dtype)
    y = jnp.einsum('bngc,gcd->bngd', d, w_c).reshape(Bn, N, D_BRANCH) * c_scale
    return y, ext[:, -C_PREFIX:]


def memory_kv(mem, g, w):
    Bn, M, _ = mem.shape
    return (rmsnorm(mem, g) @ w).reshape(Bn, M, 2, M_HEADS, M_HEAD_DIM)


def memory_attention(q, mem_kv):
    Bn, N, _ = q.shape
    qh = q.reshape(Bn, N, M_HEADS, M_HEAD_DIM)
    logits = jnp.einsum('bnhd,bmhd->bhnm', qh, mem_kv[:, :, 0]).astype(jnp.float32) * (M_HEAD_DIM ** -0.5)
    p = jax.nn.softmax(logits, axis=-1).astype(q.dtype)
    return jnp.einsum('bhnm,bmhd->bnhd', p, mem_kv[:, :, 1]).reshape(Bn, N, D_BRANCH)


def conv_ffn(h, conv_prefix, w_up, conv_w, conv_b, w_down):
    N = h.shape[1]
    up = h @ w_up
    ext = jnp.concatenate([conv_prefix, up], axis=1)
    conv = conv_b
    for t in range(CONV_W):
        conv = conv + conv_w[t] * ext[:, t:t + N]
    a, b = jnp.split(conv, 2, axis=-1)
    return (jax.nn.gelu(a, approximate=True) * b) @ w_down, ext[:, -(CONV_W - 1):]


def trunk_layer(x, a_prefixes, mem_kv, pool_prefix, conv_prefix, pos0, rel_bias,
                g_pre_mix, w_in, g_v, w_s, b_s, w_c, c_scale, w_out, g_post_mix,
                g_pre_ffn, w_up, conv_w, conv_b, w_down, g_post_ffn):
    Bn, N, _ = x.shape
    h = rmsnorm(x, g_pre_mix)
    z = h @ w_in
    y_a, a_new = dilated_mixer(z[..., COL_A:COL_B], a_prefixes, rel_bias)
    y_b, v_new = spatial_gating_mixer(z[..., COL_B:COL_C], g_v, w_s, b_s)
    y_c, pool_new = pooling_mixer(z[..., COL_C:COL_M], pool_prefix, pos0, w_c, c_scale)
    y_m = memory_attention(z[..., COL_M:COL_G], mem_kv)
    gates = jax.nn.sigmoid(z[..., COL_G:])
    merged = jnp.concatenate([y_a, y_b, y_c, y_m], axis=-1) * gates
    x = x + rmsnorm(merged @ w_out, g_post_mix)
    f, conv_new = conv_ffn(rmsnorm(x, g_pre_ffn), conv_prefix, w_up, conv_w, conv_b, w_down)
    x = x + rmsnorm(f, g_post_ffn)
    return x, a_new, v_new, pool_new, conv_new


def setup_inputs(seed: int = 0) -> dict:
    key = jax.random.key(seed)
    ks = jax.random.split(key, 32)
    f32 = jnp.float32

    def nrm(k, shape, scale):
        return jax.random.normal(k, shape, f32) * scale

    def gain(k, shape):
        return 1.0 + 0.05 * jax.random.normal(k, shape, f32)

    kvd = (2, A_HPG, A_HEAD_DIM)
    return {
        'x_prompt': nrm(ks[0], (BATCH, SEQ, D_MODEL), 1.0),
        'x_sample': nrm(ks[1], (DEC_BATCH, DEC_SEQ, D_MODEL), 1.0),
        'cache_a_w128': nrm(ks[2], (DEPTH, DEC_BATCH, min(A_GROUPS[0][0], PAST_LEN)) + kvd, 1.0),
        'cache_a_w512': nrm(ks[3], (DEPTH, DEC_BATCH, min(A_GROUPS[1][0], PAST_LEN)) + kvd, 1.0),
        'cache_a_w2048': nrm(ks[4], (DEPTH, DEC_BATCH, min(A_GROUPS[2][0], PAST_LEN)) + kvd, 1.0),
        'cache_mem_kv': nrm(ks[5], (DEPTH, DEC_BATCH, MEM_LEN, 2, M_HEADS, M_HEAD_DIM), 1.0),
        'state_pool': nrm(ks[6], (DEPTH, DEC_BATCH, C_PREFIX, D_BRANCH), 1.0),
        'state_conv': nrm(ks[7], (DEPTH, DEC_BATCH, CONV_W - 1, 2 * D_FF), 1.0),
        'mem_prompt': nrm(ks[8], (BATCH, MEM_LEN, D_MODEL), 1.0),
        'rel_bias': nrm(ks[9], (REL_BUCKETS, A_HEADS), 0.5),
        'norm_pre_mix': gain(ks[10], (DEPTH, D_MODEL)),
        'w_in': nrm(ks[11], (DEPTH, D_MODEL, IN_COLS), D_MODEL ** -0.5),
        'norm_v_b': gain(ks[12], (DEPTH, D_BRANCH)),
        'w_spatial': nrm(ks[13], (DEPTH, B_GROUPS, B_CHUNK, B_CHUNK), B_CHUNK ** -0.5),
        'b_spatial': 1.0 + 0.1 * jax.random.normal(ks[14], (DEPTH, B_GROUPS, B_CHUNK), f32),
        'w_pool': nrm(ks[15], (DEPTH, C_GROUPS, C_GROUP_DIM, C_GROUP_DIM), C_GROUP_DIM ** -0.5),
        'pool_scale': 1.0 + 0.1 * jax.random.normal(ks[16], (DEPTH, D_BRANCH), f32),
        'norm_mem': gain(ks[17], (DEPTH, D_MODEL)),
        'w_mem_kv': nrm(ks[18], (DEPTH, D_MODEL, 2 * D_BRANCH), D_MODEL ** -0.5),
        'w_out': nrm(ks[19], (DEPTH, MIX_WIDTH, D_MODEL), MIX_WIDTH ** -0.5),
        'norm_post_mix': gain(ks[20], (DEPTH, D_MODEL)),
        'norm_pre_ffn': gain(ks[21], (DEPTH, D_MODEL)),
        'w_up': nrm(ks[22], (DEPTH, D_MODEL, 2 * D_FF), D_MODEL ** -0.5),
        'conv_w': nrm(ks[23], (DEPTH, CONV_W, 2 * D_FF), CONV_W ** -0.5),
        'conv_b': nrm(ks[24], (DEPTH, 2 * D_FF), 0.02),
        'w_down': nrm(ks[25], (DEPTH, D_FF, D_MODEL), D_FF ** -0.5),
        'norm_post_ffn': gain(ks[26], (DEPTH, D_MODEL)),
    }


def reference(x_prompt, x_sample, cache_a_w128, cache_a_w512, cache_a_w2048, cache_mem_kv,
              state_pool, state_conv, mem_prompt, rel_bias, norm_pre_mix, w_in, norm_v_b,
              w_spatial, b_spatial, w_pool, pool_scale, norm_mem, w_mem_kv, w_out,
              norm_post_mix, norm_pre_ffn, w_up, conv_w, conv_b, w_down, norm_post_ffn):
    xp, xs = x_prompt, x_sample
    Bp = xp.shape[0]
    a_p = [[], [], []]
    a_s = [[], [], []]
    memkv_p, pool_p, conv_p = [], [], []
    v_s, pool_s, conv_s = [], [], []
    for l in range(DEPTH):
        lw = (rel_bias, norm_pre_mix[l], w_in[l], norm_v_b[l], w_spatial[l], b_spatial[l],
              w_pool[l], pool_scale[l], w_out[l], norm_post_mix[l], norm_pre_ffn[l],
              w_up[l], conv_w[l], conv_b[l], w_down[l], norm_post_ffn[l])
        mkv = memory_kv(mem_prompt, norm_mem[l], w_mem_kv[l])
        empty_a = [jnp.zeros((Bp, 0, 2, A_HPG, A_HEAD_DIM), xp.dtype) for _ in A_GROUPS]
        xp, an, _, pn, cn = trunk_layer(
            xp, empty_a, mkv,
            jnp.zeros((Bp, C_PREFIX, D_BRANCH), xp.dtype),
            jnp.zeros((Bp, CONV_W - 1, 2 * D_FF), xp.dtype), 0, *lw)
        for g in range(len(A_GROUPS)):
            a_p[g].append(an[g])
        memkv_p.append(mkv)
        pool_p.append(pn)
        conv_p.append(cn)
        xs, an, vn, pn, cn = trunk_layer(
            xs, [cache_a_w128[l], cache_a_w512[l], cache_a_w2048[l]], cache_mem_kv[l],
            state_pool[l], state_conv[l], PAST_LEN, *lw)
        for g in range(len(A_GROUPS)):
            a_s[g].append(an[g])
        v_s.append(vn)
        pool_s.append(pn)
        conv_s.append(cn)
    return (xp, xs,
            jnp.stack(a_p[0]), jnp.stack(a_p[1]), jnp.stack(a_p[2]),
            jnp.stack(memkv_p), jnp.stack(pool_p), jnp.stack(conv_p),
            jnp.stack(a_s[0]), jnp.stack(a_s[1]), jnp.stack(a_s[2]),
            jnp.stack(v_s), jnp.stack(pool_s), jnp.stack(conv_s))
```

```python
import numpy as np
import concourse.bass as bass
import concourse.mybir as mybir
from concourse.bass_utils import run_bass_kernel_spmd

F32 = mybir.dt.float32
BF16 = mybir.dt.bfloat16
AF = mybir.ActivationFunctionType
ALU = mybir.AluOpType

D = 2048
KC = 16
T = 512
NT = 4
SEQ = 2048
TS = 64
NB = 16
DFF = 5504
NPAIR = 43
EPS = 1e-6
NEG = -1e30
COL_A, COL_B, COL_C, COL_M, COL_G = 0, 2304, 3840, 4608, 5376
A_DIL = (1, 4, 16)

A_MERGED, A_REST, A_HT, A_SCR, A_END = 0, 24, 43, 59, 82
WSLOTS = 3
WSLOT_ELEMS = 4096

P_PRE, P_MEM, P_POSTM, P_PREF, P_POSTF = 0, 16, 32, 48, 64
P_GV, P_CS, P_CW, P_CB, NPAR = 80, 86, 92, 92 + 258, 92 + 258 + 86

SEGS = {0: [(0, 128, 0, 0, 128)], 1: [(0, 64, 0, 128, 192), (64, 128, 1, 0, 64)],
        2: [(0, 128, 1, 64, 192)], 3: [(0, 128, 2, 0, 128)],
        4: [(0, 64, 2, 128, 192), (64, 128, 3, 0, 64)], 5: [(0, 128, 3, 64, 192)]}
GSEGS = {0: [(0, 0, 128), (1, 0, 64)], 1: [(1, 64, 128), (2, 0, 128)],
         2: [(3, 0, 128), (4, 0, 64)], 3: [(4, 64, 128), (5, 0, 128)]}


class Op:
    __slots__ = ("eng", "fn", "deps", "signal", "sem", "val", "waits", "dma", "idx")


class Sched:
    COMPUTE = ("pe", "act", "dve")
    QUEUES = ("sp", "pool")

    def __init__(self):
        self.ops = []
        self.last_w = {}
        self.readers = {}
        self.final = []

    def add(self, eng, fn, reads=(), writes=(), out=False):
        op = Op()
        op.eng, op.fn, op.signal, op.dma, op.idx = eng, fn, False, eng in self.QUEUES, len(self.ops)
        deps = {}
        psr = [r for r in reads if isinstance(r, tuple) and r[0] == "ps"]
        if psr:
            writes = list(writes) + [r for r in psr if r not in writes]

        def dep(o):
            if o is None:
                return
            if o.dma:
                deps[("d", o.idx)] = o
            else:
                k = ("c", o.eng)
                if k not in deps or deps[k].idx < o.idx:
                    deps[k] = o

        for r in reads:
            dep(self.last_w.get(r))
        for w in writes:
            dep(self.last_w.get(w))
            for o in self.readers.get(w, {}).values():
                dep(o)
        for r in reads:
            rd = self.readers.setdefault(r, {})
            rd[("d", op.idx) if op.dma else ("c", eng)] = op
        for w in writes:
            self.last_w[w] = op
            self.readers[w] = {}
        op.deps = [o for o in deps.values() if not (o.eng == "pe" and eng == "pe")]
        for o in op.deps:
            o.signal = True
        self.ops.append(op)
        if out:
            self.final.append(op)
        return op

    def finish(self, nc, sems, dma_sems):
        fin = Op()
        fin.eng, fin.fn, fin.signal, fin.dma, fin.idx = "sp", None, False, True, len(self.ops)
        fin.deps = list(self.final)
        for o in fin.deps:
            o.signal = True
        self.ops.append(fin)
        cnt = {e: 0 for e in self.COMPUTE}
        dcnt = {q: 0 for q in self.QUEUES}
        dsem_val = {}
        waited = {e: {} for e in self.COMPUTE + self.QUEUES}
        for op in self.ops:
            waits = []

            def need(sem, val, _w=waits, _e=op.eng):
                if waited[_e].get(id(sem), 0) < val:
                    waited[_e][id(sem)] = val
                    _w.append((sem, val))

            for o in op.deps:
                need(o.sem, o.val)
            if op.fn is not None:
                if op.dma:
                    pool = dma_sems[op.eng]
                    s = pool[dcnt[op.eng] % len(pool)]
                    dcnt[op.eng] += 1
                    prev = dsem_val.get(id(s), 0)
                    if prev:
                        need(s, prev)
                    op.sem, op.val = s, prev + 16
                    dsem_val[id(s)] = prev + 16
                    op.signal = True
                elif op.signal:
                    cnt[op.eng] += 1
                    op.sem, op.val = sems[op.eng], cnt[op.eng]
            op.waits = waits

    def emit(self, engname, eng):
        for op in self.ops:
            if op.eng != engname:
                continue
            for sem, val in op.waits:
                eng.wait_ge(sem, val)
            if op.fn is None:
                continue
            ins = op.fn(eng)
            if op.signal:
                ins.then_inc(op.sem, 16 if op.dma else 1)


def _bf16_bits_placeholder():
    return None


def build_program(stage=99):
    nc = bass.Bass("TRN2", target_bir_lowering=False)
    S = Sched()

    def din(name, shape, dt=F32):
        return nc.dram_tensor(name, list(shape), dt, kind="ExternalInput").ap()

    def dout(name, shape, dt=F32):
        return nc.dram_tensor(name, list(shape), dt, kind="ExternalOutput").ap()

    def dscr(name, shape, dt):
        return nc.dram_tensor(name, list(shape), dt, kind="Internal").ap()

    xpT = din("xpT", [128, KC, SEQ])
    xsT = din("xsT", [128, KC, TS])
    mpT = din("mpT", [128, KC, 256])
    SMALL = globals().get("DBG_SMALL", False)
    wst = din("wst", [2, 128, 448512 if not SMALL else 8])
    wmk = din("wmk", [2, 128, 24576])
    par = din("par", [128, 2, NPAR])
    par2 = din("par2", [128, 2, NPAR])
    wsT_d = din("wsT", [2, 128, 4, 128])
    bs_d = din("bsr", [2, 1, 4, 128])
    wc_d = din("wcp", [2, 128, 6, 192])
    relb = din("relb", [32, 6])
    cst = din("cst", [128, 1536])
    caK_d = din("caK", [2, NB, 128, 2304 if not SMALL else 8])
    caV_d = din("caV", [2, NB, 128, 2304 if not SMALL else 8])
    cmK_d = din("cmK", [2, NB, 128, 1536 if not SMALL else 8])
    cmV_d = din("cmV", [2, NB, 128, 1536 if not SMALL else 8])
    spT_d = din("spT", [2, 128, 6, NB, 15])
    scT_d = din("scT", [2, 128, 86, NB, 2])
    wrep_d = din("wrep", [64, 2, 4, 64])
    brep_d = din("brep", [1, 2, 4, 64])
    ypT = dout("ypT", [128, KC, SEQ])
    ysT = dout("ysT", [128, KC, TS])
    oa = [dout("oa0", [2, 128, 512]), dout("oa1", [2, 512, 512]), dout("oa2", [2, 2048, 512])]
    omkv = dout("omkv", [2, 256, 1536])
    opool = dout("opool", [2, 128, 6, 15])
    oconv = dout("oconv", [2, 128, 86, 2])
    osa = [dout("osa%d" % g, [2, TS, 512]) for g in range(3)]
    osv = dout("osv", [2, 128, 6, TS])
    ospool = dout("ospool", [2, 128, 6, NB, 15])
    osconv = dout("osconv", [2, 128, 86, NB, 2])
    vscr = dscr("vscr", [2, SEQ, 768], BF16)
    mkscr = dscr("mkscr", [2, 128, 3072], BF16)
    wscr = dscr("wscr", [2, 128, 448512], BF16)

    import contextlib
    es = contextlib.ExitStack()
    with es:
        def sb(name, shape, dt):
            return es.enter_context(nc.sbuf_tensor(name, list(shape), dt))

        xT = sb("xT", [128, KC, T], F32)
        arena = sb("arena", [128, A_END * 512], BF16)
        kvreg = sb("kvreg", [128, 16384], BF16)
        wbuf = sb("wbuf", [128, WSLOTS, WSLOT_ELEMS], BF16)
        toep = sb("toep", [128, 6, 256], F32)
        memkv = sb("memkv", [128, 3072], BF16)
        prm = sb("prm", [128, 2, NPAR], F32)
        wsT = sb("wsTb", [128, 2, 4, 128], BF16)
        wcb = sb("wcb", [128, 2, 6, 192], BF16)
        bsb = sb("bsb", [1, 2, 4, 128], BF16)
        cs = sb("cs", [128, 1536], F32)
        ident = sb("ident", [128, 128], BF16)
        ones_b = sb("ones_b", [128, 128], BF16)
        ones_f = sb("ones_f", [128, 128], F32)
        ppref = sb("ppref", [128, 2, 6, 15], F32)
        cpref = sb("cpref", [128, 2, 86, 2], F32)
        wbig = sb("wbig", [64, 2, 4, 64], BF16)
        bsrow = sb("bsrow", [1, 2, 4, 64], BF16)
        bnew = sb("bnew", [64, 6, 64], F32)
        bc0 = sb("bc0", [128, 2, 64], F32)
        psb = [es.enter_context(nc.psum_tensor("ps%d" % i, [128, 512], F32)) for i in range(8)]
        sem = {e: es.enter_context(nc.semaphore("s_" + e)) for e in Sched.COMPUTE}
        dsem = {q: [es.enter_context(nc.semaphore("d_%s%d" % (q, i))) for i in range(12)]
                for q in Sched.QUEUES}

        arena_f = arena[:].bitcast(F32)

        def ar_keys(slot0, nslots):
            return [("ar", s) for s in range(int(slot0), int(slot0 + nslots))]

        class Reg:
            def __init__(self, slot0, elems, dt):
                self.slot0, self.elems, self.dt = slot0, elems, dt
                self.bytes = elems * (4 if dt == F32 else 2)
                assert self.bytes % 1024 == 0 or True
                self.nslots = (self.bytes + 1023) // 1024
                assert slot0 + self.nslots <= A_END, (slot0, self.nslots)

            def ap(self, lo=0, hi=None, p0=0, p1=128):
                hi = self.elems if hi is None else hi
                if self.dt == F32:
                    b = self.slot0 * 256
                    return arena_f[p0:p1, b + lo:b + hi]
                b = self.slot0 * 512
                return arena[p0:p1, b + lo:b + hi]

            def keys(self, lo=0, hi=None):
                hi = self.elems if hi is None else hi
                esz = 4 if self.dt == F32 else 2
                s0 = self.slot0 + (lo * esz) // 1024
                s1 = self.slot0 + (hi * esz + 1023) // 1024
                return [("ar", s) for s in range(s0, s1)]

        class Alloc:
            def __init__(self, slot0, slot1):
                self.cur, self.end = slot0, slot1

            def get(self, elems, dt):
                r = Reg(self.cur, elems, dt)
                self.cur += r.nslots
                assert self.cur <= self.end, "arena scratch overflow"
                return r

        PS = lambda b: ("ps", b)

        def mm(out, lhsT, rhs, start, stop, reads, writes):
            S.add("pe", lambda e: e.matmul(out, lhsT=lhsT, rhs=rhs, start=start, stop=stop,
                                           skip_group_check=True), reads, writes)

        def tr(out, in_, idn, reads, writes):
            S.add("pe", lambda e: e.transpose(out, in_, idn), reads, writes)

        def act(out, in_, func, reads, writes, bias=None, scale=None):
            kw = {}
            if bias is not None:
                kw["bias"] = bias
            if scale is not None:
                kw["scale"] = scale
            S.add("act", lambda e: e.activation(out=out, in_=in_, func=func, **kw), reads, writes)

        def dve_copy(out, in_, reads, writes):
            S.add("dve", lambda e: e.tensor_copy(out=out, in_=in_), reads, writes)

        def dve_tt(out, a, b, op, reads, writes):
            S.add("dve", lambda e: e.tensor_tensor(out=out, in0=a, in1=b, op=op), reads, writes)

        def dve_ts(out, a, s1, s2, op0, op1, reads, writes):
            S.add("dve", lambda e: e.tensor_scalar(out=out, in0=a, scalar1=s1, scalar2=s2, op0=op0,
                                                   op1=op1), reads, writes)

        def dve_stt(out, a, s, b, op0, op1, reads, writes):
            S.add("dve", lambda e: e.scalar_tensor_tensor(out=out, in0=a, scalar=s, in1=b, op0=op0,
                                                          op1=op1), reads, writes)

        def dve_recip(out, in_, reads, writes):
            S.add("dve", lambda e: e.reciprocal(out=out, in_=in_), reads, writes)

        def dve_memset(out, val, writes):
            S.add("dve", lambda e: e.memset(out, val), (), writes)

        def dma(q, out, in_, reads, writes, is_out=False):
            S.add(q, lambda e: e.dma_start(out=out, in_=in_), reads, writes, out=is_out)

        wstate = {"n": 0}

        def wload(src_ap, nelem):
            slot = wstate["n"] % WSLOTS
            wstate["n"] += 1
            assert nelem <= WSLOT_ELEMS
            dma("pool", wbuf[:, slot, 0:nelem], src_ap, (), [("w", slot)])
            return slot

        def wview(slot, nk, ncol):
            return wbuf[:, slot, 0:nk * ncol].rearrange("p (k n) -> p k n", k=nk)

        W_IN_OFF = 0
        W_OUT_OFF = 135168
        W_UP_OFF = W_OUT_OFF + 49152
        W_DN_OFF = W_UP_OFF + 176128

        C_ID, C_ONES, C_TRIL, C_BK, C_FAC, C_MB, C_MN = 0, 128, 256, 384, 1152, 1216, 1280
        dma("sp", cs[:, :], cst[:, :], (), ["cs"])
        dma("sp", prm[:, :, :], par[:, :, :], (), ["prm"])
        dve_copy(ident[:, :], cs[:, C_ID:C_ID + 128], ["cs"], ["ident"])
        dve_copy(ones_b[:, :], cs[:, C_ONES:C_ONES + 128], ["cs"], ["ones_b"])
        dve_copy(ones_f[:, :], cs[:, C_ONES:C_ONES + 128], ["cs"], ["ones_f"])
        dve_memset(ppref[:, :, :, :], 0.0, ["ppref"])
        dve_memset(cpref[:, :, :, :], 0.0, ["cpref"])

        CUT = globals().get('SETUP_CUT', 99)
        sa = Alloc(A_SCR, A_END)
        r_ws = sa.get(2 * 512, F32)
        r_wc = sa.get(2 * 1152, F32)
        r_bs = sa.get(1024, F32)
        dma("sp", r_ws.ap().rearrange("p (l g i) -> p l g i", l=2, g=4),
            wsT_d.rearrange("l p g i -> p l g i"), (), r_ws.keys())
        dma("sp", r_wc.ap().rearrange("p (l c d) -> p l c d", l=2, c=6),
            wc_d.rearrange("l p c d -> p l c d"), (), r_wc.keys())
        dma("sp", r_bs.ap(p0=0, p1=1).rearrange("p (l g i) -> p l g i", l=2, g=4),
            bs_d.rearrange("l p g i -> p l g i"), (), r_bs.keys())
        for l in range(2):
            for g in range(4):
                dve_tt(wsT[:, l, g, :], r_ws.ap(l * 512 + g * 128, l * 512 + g * 128 + 128),
                       cs[:, C_TRIL:C_TRIL + 128], ALU.mult, r_ws.keys() + ["cs"], ["wsT"])
        dve_copy(wcb[:, :, :, :], r_wc.ap().rearrange("p (l c d) -> p l c d", l=2, c=6), r_wc.keys(), ["wcb"])
        dve_copy(bsb[:, :, :, :], r_bs.ap(p0=0, p1=1).rearrange("p (l g i) -> p l g i", l=2, g=4),
                 r_bs.keys(), ["bsb"])

        r_rb = sa.get(256, F32)
        dma("sp", r_rb.ap(0, 192), bass.AP(relb.tensor, 0, [[0, 128], [1, 192]]), (), r_rb.keys())
        r_tmp = sa.get(256, F32)
        BUCKETS = _bucket_tables()
        toep_state = {"done": False}

        def build_toeplitz():
            if toep_state["done"]:
                return
            toep_state["done"] = True
            for h in range(6 if CUT >= 2 else 0):
                g = h // 2
                bk = cs[:, C_BK + 256 * g:C_BK + 256 * g + 256]
                dve_ts(toep[:, h, :], bk, -0.5, NEG, ALU.is_lt, ALU.mult, ["cs"], [("toep", h)])
                for b in BUCKETS[g]:
                    dve_ts(r_tmp.ap(), bk, float(b), r_rb.ap(b * 6 + h, b * 6 + h + 1), ALU.is_equal, ALU.mult,
                           ["cs"] + r_rb.keys(), r_tmp.keys())
                    dve_tt(toep[:, h, :], toep[:, h, :], r_tmp.ap(), ALU.add, r_tmp.keys() + [("toep", h)],
                           [("toep", h)])
            for h in range(6 if CUT >= 3 else 0):
                mcol = C_MN + (0 if h < 2 else 64)
                dve_tt(bnew[:, h, :], toep[0:64, h, 128:192], cs[0:64, mcol:mcol + 64], ALU.add,
                       [("toep", h), "cs"], ["bnew"])
            sbias = bc0[:, 0, 0:24].rearrange("p (h t) -> p h t", t=4)
            for h in range(6 if CUT >= 3 else 0):
                if h < 2:
                    dve_copy(sbias[:, h, :], toep[:, h, 0:4], [("toep", h)], ["bc0"])
                else:
                    dve_copy(sbias[:, h, :], toep[:, h, 0:1].to_broadcast([128, 4]), [("toep", h)], ["bc0"])


        def rms_stats(src_chunk, src_keys, ncol, sc, out_rstd):
            acc = sc.get(512, F32)
            sq = [sc.get(512, F32), sc.get(512, F32)]
            for c in range(KC):
                if c == 0:
                    act(acc.ap(0, ncol), src_chunk(c), AF.Square, src_keys(c), acc.keys())
                else:
                    t = sq[c % 2]
                    act(t.ap(0, ncol), src_chunk(c), AF.Square, src_keys(c), t.keys())
                    dve_tt(acc.ap(0, ncol), acc.ap(0, ncol), t.ap(0, ncol), ALU.add, acc.keys() + t.keys(), acc.keys())
            mm(psb[7][:, 0:ncol], ones_f[:, :], acc.ap(0, ncol), True, True, acc.keys() + ["ones_f"], [PS(7)])
            act(acc.ap(0, ncol), psb[7][:, 0:ncol], AF.Sqrt, [PS(7)], acc.keys(), bias=EPS_AP(), scale=1.0 / D)
            dve_recip(out_rstd.ap(0, ncol), acc.ap(0, ncol), acc.keys(), out_rstd.keys())

        eps_t = sb("eps_t", [128, 1], F32)
        dve_memset(eps_t[:, :], EPS, ["eps"])
        EPS_AP = lambda: eps_t[:, 0:1]

        def xs_chunk(ncol):
            return (lambda c: xT[:, c, 0:ncol]), (lambda c: [("x", c)])

        R_HT = Reg(A_HT, KC * 512, BF16)
        R_MG = Reg(A_MERGED, 43 * 512, BF16)
        R_OF = Reg(A_HT, KC * 512, F32)

        def hT(c, ncol, p0=0, p1=128):
            return R_HT.ap(c * 512, c * 512 + ncol, p0, p1)

        def hTk(c):
            return R_HT.keys(c * 512, c * 512 + 512)

        def mg(c, ncol, p0=0, p1=128):
            return R_MG.ap(c * 512, c * 512 + ncol, p0, p1)

        def mgk(c):
            return R_MG.keys(c * 512, c * 512 + 512)

        def norm_to_h(l, pcol, ncol, sc_slot0):
            sc = Alloc(sc_slot0, A_END)
            rstd = sc.get(512, F32)
            ch, ck = xs_chunk(ncol)
            rms_stats(ch, ck, ncol, sc, rstd)
            for c in range(KC):
                dve_stt(hT(c, ncol), xT[:, c, 0:ncol], prm[:, l, pcol + c:pcol + c + 1], rstd.ap(0, ncol),
                        ALU.mult, ALU.mult, [("x", c), "prm"] + rstd.keys(), hTk(c))

        def dense_fm(l, woff, nk, nout, rhs_fn, rhs_keys, ncol, consume, cols_per_blk=256, kc_split=1,
                     blk_order=None, blk_src=None):
            pass

        def layer(l, ti, ncol, sample):
            pend_box = []
            layer_body(l, ti, ncol, sample, pend_box)
            for fn in pend_box:
                fn()

        def layer_body(l, ti, ncol, sample, pend_box):
            wl = wst[l]
            bank = {"n": 0}

            def nextbank():
                b = bank["n"] % 4
                bank["n"] += 1
                return b

            blk_no = {"n": 0}

            def wblock(off, nelem):
                conv_tile = blk_no["n"] % 2
                blk_no["n"] += 1
                use_scr = sample or ti > conv_tile
                first_pass = (not sample) and ti == conv_tile
                me = blk_no["n"]
                while wb_pend and wb_pend[0].blk <= me - 2:
                    wb_pend.pop(0)()
                if use_scr:
                    slot = wstate["n"] % WSLOTS
                    wstate["n"] += 1
                    dma("pool", wbuf[:, slot, 0:nelem], wscr[l][:, off:off + nelem], [("wscr", l, off)], [("w", slot)])
                    return slot
                slot = wload(wl[:, off:off + nelem], nelem)
                if first_pass:
                    def wb(off=off, nelem=nelem, slot=slot):
                        dma("pool", wscr[l][:, off:off + nelem], wbuf[:, slot, 0:nelem], [("w", slot)], [("wscr", l, off)])
                    wb.blk = me
                    wb_pend.append(wb)
                return slot

            wb_pend = pend_box

            if sample:
                for k7 in range(7):
                    n14 = min(14, 86 - 14 * k7)
                    dma("sp", xT[:, 3 + k7, 64:64 + 32 * n14],
                        scT_d[l, :, 14 * k7:14 * k7 + n14, :, :].rearrange("p c b r -> p (c b r)"), (), [("x", 3 + k7)])
            norm_to_h(l, P_PRE, ncol, A_SCR)

            sc = Alloc(A_REST, A_HT)
            qT = sc.get(6 * 512, BF16)
            ybuf = sc.get(6 * 512, F32)
            sc2 = Alloc(A_SCR, A_SCR + 6)
            s0 = ti * T

            def w_in_block(j):
                return wblock(W_IN_OFF + j * 4096, 4096)

            def fm_chunks(jblk, handler, first_chunk):
                slot = w_in_block(jblk)
                wv = wview(slot, KC, 256)
                for q in range(2):
                    b = nextbank()
                    for k in range(KC):
                        mm(psb[b][:, 0:ncol], wv[:, k, q * 128:(q + 1) * 128], hT(k, ncol), k == 0, k == KC - 1,
                           [("w", slot)] + hTk(k), [PS(b)])
                    handler(first_chunk + q, b)

            ntb = (ncol + 127) // 128
            tbw = min(ncol, 128)
            kst = [sc2.get(256, F32) for _ in range(4)]
            kbf = [sc2.get(256, BF16) for _ in range(2)]
            KT_OFF = l * 7424
            KT_G = (0, 1280, 3328)
            KT_W = (640, 1024, 2048)

            vsn = Reg(A_END - 2, 768, BF16) if sample else None

            def kt_ap(g, s, lo, hi, step=1):
                base = KT_OFF + KT_G[g] + s * KT_W[g]
                if sample:
                    base = 16000 + (2 * g + s) * 64
                if step == 1:
                    return kvreg[:, base + lo:base + hi]
                n = (hi - lo + step - 1) // step
                q_, r_ = divmod(lo, step)
                return kvreg[:, base + q_ * step:base + (q_ + n) * step].rearrange("p (n s) -> p n s", s=step)[:, :, r_]

            def kt_cur0(g):
                if sample:
                    return 0
                return (128, 512, s0)[g]

            if not sample and ti > 0:
                for s in range(2):
                    dve_copy(kt_ap(0, s, 0, 128), kt_ap(0, s, 512, 640), [("kt", l, 0)], [("kt", l, 0)])
                    act(kt_ap(1, s, 0, 512), kt_ap(1, s, 512, 1024), AF.Copy, [("kt", l, 1)], [("kt", l, 1)])
            cnt = 0
            for j in range(6):
                isv, g = j >= 3, j % 3
                slot = w_in_block(j)
                wv = wview(slot, KC, 256)
                for tb in range(ntb):
                    b = nextbank()
                    for k in range(KC):
                        mm(psb[b][0:tbw, 0:256], hT(k, ncol)[:, tb * 128:tb * 128 + tbw], wv[:, k, :],
                           k == 0, k == KC - 1, [("w", slot)] + hTk(k), [PS(b)])
                    st = kst[cnt % 4]
                    kb = kbf[cnt % 2]
                    cnt += 1
                    tok0 = s0 + tb * 128
                    if sample:
                        act(st.ap(p1=tbw), psb[b][0:tbw, 0:256], AF.Copy, [PS(b)], st.keys())
                        dma("sp", osa[g][l, :, (256 if isv else 0):(256 if isv else 0) + 256],
                            st.ap(p1=tbw), st.keys(), [], is_out=True)
                    else:
                        keep = (128, 512, 2048)[g]
                        if tok0 >= SEQ - keep:
                            act(st.ap(), psb[b][:, 0:256], AF.Copy, [PS(b)], st.keys())
                            r0 = tok0 - (SEQ - keep)
                            dma("sp", oa[g][l, r0:r0 + 128, (256 if isv else 0):(256 if isv else 0) + 256],
                                st.ap(), st.keys(), [], is_out=True)
                    if isv:
                        if sample:
                            dve_copy(vsn.ap(g * 256, (g + 1) * 256, p1=tbw), psb[b][0:tbw, 0:256], [PS(b)], vsn.keys())
                        else:
                            dve_copy(kb.ap(), psb[b][:, 0:256], [PS(b)], kb.keys())
                            dma("sp", vscr[l, tok0:tok0 + 128, g * 256:(g + 1) * 256], kb.ap(), kb.keys(),
                                [("vscr", l, ti)])
                    else:
                        dve_copy(kb.ap(p1=tbw), psb[b][0:tbw, 0:256], [PS(b)], kb.keys())
                        pt = psb[4 + (cnt % 2)]
                        ptb = pt[:].bitcast(BF16)
                        for s in range(2):
                            tr(ptb[:, s * 128:s * 128 + tbw], kb.ap(s * 128, s * 128 + 128, p1=tbw), ident[0:tbw, 0:tbw],
                               kb.keys() + ["ident"], [PS(4 + (cnt % 2))])
                        c0 = kt_cur0(g) + tb * 128
                        for s in range(2):
                            act(kt_ap(g, s, c0, c0 + tbw), ptb[:, s * 128:s * 128 + tbw], AF.Copy,
                                [PS(4 + (cnt % 2))], [("kt", l, g)] if not sample else ["ktnew"])

            def q_handler(c, b):
                act(qT.ap(c * 512, c * 512 + ncol), psb[b][:, 0:ncol], AF.Copy, [PS(b)], qT.keys(c * 512, c * 512 + 512))
            for j in range(3):
                fm_chunks(6 + j, q_handler, 2 * j)

            build_toeplitz()
            if sample:
                sample_attention(l, qT, ybuf, Alloc(A_SCR + 6, A_END - 2), kt_ap, vsn)
            else:
                prompt_attention(l, ti, qT, ybuf, Alloc(A_SCR + 6, A_END), kt_ap)

            gsc = Alloc(A_END - 4, A_END)
            gt = [gsc.get(512, F32), gsc.get(512, F32)]
            gstate = {"n": 0}

            def gate_blocks(jblk0, mg0):
                def gh(c, b):
                    t = gt[gstate["n"] % 2]
                    gstate["n"] += 1
                    act(t.ap(0, ncol), psb[b][:, 0:ncol], AF.Sigmoid, [PS(b)], t.keys())
                    dve_tt(mg(mg0 + c, ncol), ybuf.ap(c * 512, c * 512 + ncol), t.ap(0, ncol), ALU.mult,
                           ybuf.keys(c * 512, c * 512 + 512) + t.keys(), mgk(mg0 + c))
                for j in range(3):
                    fm_chunks(jblk0 + j, gh, 2 * j)

            gate_blocks(9, 0)
            if stage < 2:
                return
            if stage == 13:
                scM = None
            branch_B(l, ncol, sample, fm_chunks, ybuf, Alloc(A_SCR, A_END - 4), Alloc(A_REST, A_REST + 6), nextbank)
            gate_blocks(18, 6)
            if stage == 11:
                return
            branch_C(l, ti, ncol, sample, fm_chunks, ybuf, Alloc(A_SCR, A_END - 4), Alloc(A_REST, A_REST + 6), nextbank)
            gate_blocks(24, 12)
            if stage == 12:
                return
            branch_M(l, ti, ncol, sample, fm_chunks, ybuf, Alloc(A_SCR, A_END - 4), Alloc(A_REST, A_REST + 6), nextbank)
            if stage in (14, 15):
                return
            gate_blocks(30, 18)
            if stage < 3:
                return

            def add_norm(woff, nk, ksplit, rhs, rhs_keys, pcol):
                acc = Reg(75, 512, F32)
                sqs = [Reg(77, 512, F32), Reg(77, 512, F32)]
                rstd = Reg(79, 512, F32)
                nkb = (nk + ksplit - 1) // ksplit
                off = woff
                for m in range(KC):
                    b = nextbank()
                    k0 = 0
                    for part in range(ksplit):
                        kn = min(nkb, nk - k0)
                        slot = wblock(off, kn * 128)
                        off += kn * 128
                        wv = wview(slot, kn, 128)
                        for k in range(kn):
                            mm(psb[b][:, 0:ncol], wv[:, k, :], rhs(k0 + k), (k0 + k) == 0, (k0 + k) == nk - 1,
                               [("w", slot)] + rhs_keys(k0 + k), [PS(b)])
                        k0 += kn
                    act(R_OF.ap(m * 512, m * 512 + ncol), psb[b][:, 0:ncol], AF.Copy, [PS(b)],
                        R_OF.keys(m * 512, m * 512 + 512))
                    if m == 0:
                        act(acc.ap(0, ncol), psb[b][:, 0:ncol], AF.Square, [PS(b)], acc.keys())
                    else:
                        t = sqs[m % 2]
                        act(t.ap(0, ncol), psb[b][:, 0:ncol], AF.Square, [PS(b)], t.keys())
                        dve_tt(acc.ap(0, ncol), acc.ap(0, ncol), t.ap(0, ncol), ALU.add, acc.keys() + t.keys(),
                               acc.keys())
                mm(psb[7][:, 0:ncol], ones_f[:, :], acc.ap(0, ncol), True, True, acc.keys() + ["ones_f"], [PS(7)])
                act(acc.ap(0, ncol), psb[7][:, 0:ncol], AF.Sqrt, [PS(7)], acc.keys(), bias=EPS_AP(), scale=1.0 / D)
                dve_recip(rstd.ap(0, ncol), acc.ap(0, ncol), acc.keys(), rstd.keys())
                for m in range(KC):
                    t = sqs[m % 2]
                    dve_stt(t.ap(0, ncol), R_OF.ap(m * 512, m * 512 + ncol), prm[:, l, pcol + m:pcol + m + 1],
                            rstd.ap(0, ncol), ALU.mult, ALU.mult,
                            R_OF.keys(m * 512, m * 512 + 512) + ["prm"] + rstd.keys(), t.keys())
                    dve_tt(xT[:, m, 0:ncol], xT[:, m, 0:ncol], t.ap(0, ncol), ALU.add, [("x", m)] + t.keys(),
                           [("x", m)])

            add_norm(W_OUT_OFF, 24, 1, lambda k: mg(k, ncol), mgk, P_POSTM)
            if stage < 4:
                return
            norm_to_h(l, P_PREF, ncol, A_REST)
            scF = Alloc(A_SCR, A_END)
            ua = [scF.get(1024, F32), scF.get(1024, F32)]
            tt = [scF.get(512, F32) for _ in range(4)]
            nb_ = NB if sample else 1
            tpb = ncol // nb_
            W = tpb + 2
            for j in range(NPAIR):
                slot = wblock(W_UP_OFF + j * 4096, 4096)
                wv = wview(slot, KC, 256)
                res = []
                for ab in range(2):
                    ch = j + ab * NPAIR
                    b = nextbank()
                    for k in range(KC):
                        mm(psb[b][:, 0:ncol], wv[:, k, ab * 128:(ab + 1) * 128], hT(k, ncol), k == 0, k == KC - 1,
                           [("w", slot)] + hTk(k), [PS(b)])
                    u = ua[ab]
                    uv = u.ap(0, nb_ * W).rearrange("p (b w) -> p b w", w=W)
                    pv = psb[b][:, 0:ncol].rearrange("p (b t) -> p b t", t=tpb)
                    cw = lambda t_: prm[:, l, P_CW + t_ * 86 + ch:P_CW + t_ * 86 + ch + 1]
                    cbias = prm[:, l, P_CB + ch:P_CB + ch + 1]
                    t1 = tt[2 * ab]
                    t1v = t1.ap(0, ncol).rearrange("p (b t) -> p b t", t=tpb)
                    if sample:
                        gc_, go_ = 3 + ch // 14, 64 + (ch % 14) * 32
                        gap = xT[:, gc_, go_:go_ + 32].rearrange("p (b r) -> p b r", r=2)
                        act(uv[:, :, 0:2], gap, AF.Copy, [("x", gc_)], u.keys())
                    else:
                        act(uv[:, :, 0:2], cpref[:, l, ch:ch + 1, :], AF.Copy, [("cpref", l, ch)], u.keys())
                    act(uv[:, :, 2:W], pv, AF.Copy, [PS(b)], u.keys())
                    act(t1v, pv, AF.Identity, [PS(b), "prm"], t1.keys(), bias=cbias, scale=cw(2))
                    dve_stt(t1v, uv[:, :, 1:W - 1], cw(1), t1v, ALU.mult, ALU.add, u.keys() + t1.keys() + ["prm"],
                            t1.keys())
                    dve_stt(t1v, uv[:, :, 0:W - 2], cw(0), t1v, ALU.mult, ALU.add, u.keys() + t1.keys() + ["prm"],
                            t1.keys())
                    if sample:
                        act(gap, uv[:, :, W - 2:W], AF.Copy, u.keys(), [("x", gc_)])
                    else:
                        act(cpref[:, l, ch:ch + 1, :], uv[:, :, W - 2:W], AF.Copy, u.keys(), [("cpref", l, ch)])
                    res.append(t1)
                ga = tt[1]
                act(ga.ap(0, ncol), res[0].ap(0, ncol), AF.Gelu_apprx_tanh, res[0].keys(), ga.keys())
                dve_tt(mg(j, ncol), ga.ap(0, ncol), res[1].ap(0, ncol), ALU.mult, ga.keys() + res[1].keys(), mgk(j))
            if not sample and ti == NT - 1:
                dma("sp", oconv[l], cpref[:, l, :, :], [("cpref", l, c) for c in range(86)], [], is_out=True)
            if sample:
                for k7 in range(7):
                    n14 = min(14, 86 - 14 * k7)
                    dma("sp", osconv[l, :, 14 * k7:14 * k7 + n14, :, :].rearrange("p c b r -> p (c b r)"),
                        xT[:, 3 + k7, 64:64 + 32 * n14], [("x", 3 + k7)], [], is_out=True)
            if stage < 5:
                return
            add_norm(W_DN_OFF, NPAIR, 2, lambda k: mg(k, ncol), mgk, P_POSTF)

        def prompt_attention(l, ti, qT, ybuf, sc2, kt_ap):
            s0 = ti * T
            vt = sc2.get(16 * 256, BF16)
            stmp = [sc2.get(256, F32), sc2.get(256, F32)]
            ptb = [sc2.get(256, BF16), sc2.get(256, BF16)]
            rden = sc2.get(1024, F32)
            DEN = (4, 5)
            vl = vscr[l]
            ucnt = {"n": 0}
            den_started = [False, False]

            def vrows(tile_idx, r0, step, nk, g):
                src = bass.AP(vl.tensor, vl.offset + r0 * 768 + g * 256, [[step * 768, nk], [1, 256]])
                t0, t1 = r0 // T, (r0 + step * (nk - 1)) // T
                dma("sp", vt.ap(tile_idx * 256, tile_idx * 256 + 256, p1=nk), src,
                    [("vscr", l, t) for t in range(t0, t1 + 1)], vt.keys(tile_idx * 256, tile_idx * 256 + 256))

            def unit(g, s, kparts, qap, nq, num_out, den_out, first_num):
                h = 2 * g + s
                i = ucnt["n"] % 2
                ucnt["n"] += 1
                sb_ = 6 + i
                col = 0
                spans = []
                for (kap, nk, bias, vi) in kparts:
                    mm(psb[sb_][0:nk, col:col + nq], kap, qap, True, True, [("kt", l, g)] + qT.keys(), [PS(sb_)])
                    dve_stt(stmp[i].ap(col, col + nq, p1=nk), psb[sb_][0:nk, col:col + nq], 128.0 ** -0.5, bias,
                            ALU.mult, ALU.add, [PS(sb_), ("toep", h)], stmp[i].keys())
                    act(ptb[i].ap(col, col + nq, p1=nk), stmp[i].ap(col, col + nq, p1=nk), AF.Exp, stmp[i].keys(),
                        ptb[i].keys())
                    spans.append((col, nk, vi))
                    col += nq
                def pv_phase():
                    for n_, (c0, nk, vi) in enumerate(spans):
                        mm(num_out, vt.ap(vi * 256 + s * 128, vi * 256 + s * 128 + 128, p1=nk), ptb[i].ap(c0, c0 + nq, p1=nk),
                           first_num and n_ == 0, True, vt.keys(vi * 256, vi * 256 + 256) + ptb[i].keys(), [PS(s)])
                        mm(den_out, ones_b[0:nk, :], ptb[i].ap(c0, c0 + nq, p1=nk), not den_started[s], True,
                           ["ones_b"] + ptb[i].keys(), [PS(DEN[s])])
                        den_started[s] = True
                while pend:
                    pend.pop(0)()
                pend.append(pv_phase)

            pend = []

            def flush():
                while pend:
                    pend.pop(0)()

            def colsel(ap2d, start, step, n):
                if step == 1:
                    return ap2d[:, start:start + n]
                q_, r_ = divmod(start, step)
                return ap2d[:, q_ * step:(q_ + n) * step].rearrange("p (n s) -> p n s", s=step)[:, :, r_]

            for g in range(3):
                d = A_DIL[g]
                if g == 0:
                    for k in range(5):
                        r0 = s0 - 128 + 128 * k
                        if r0 >= 0:
                            vrows(k, r0, 1, 128, 0)
                elif g == 1:
                    for r in range(4):
                        if ti > 0:
                            vrows(5 + 2 * r, s0 - 512 + r, 4, 128, 1)
                        vrows(5 + 2 * r + 1, s0 + r, 4, 128, 1)
                else:
                    for r in range(16):
                        vrows(r, r, 16, 32 * (ti + 1), 2)
                for s in range(2):
                    h = 2 * g + s
                    numb = psb[s]
                    denb = psb[DEN[s]]
                    qh = qT.ap(h * 512, h * 512 + 512)
                    first = True
                    if g == 0:
                        for qb in range(4):
                            parts = []
                            if ti > 0 or qb > 0:
                                parts.append((kt_ap(0, s, 128 * qb, 128 * qb + 128), 128, toep[:, h, 0:128], qb))
                            parts.append((kt_ap(0, s, 128 * qb + 128, 128 * qb + 256), 128, toep[:, h, 128:256], qb + 1))
                            unit(g, s, parts, qh[:, qb * 128:qb * 128 + 128], 128, numb[:, qb * 128:qb * 128 + 128],
                                 denb[:, qb * 128:qb * 128 + 128], first)
                            first = False
                    elif g == 1:
                        for r in range(4):
                            parts = []
                            if ti > 0:
                                parts.append((kt_ap(1, s, r, 512, 4), 128, toep[:, h, 0:128], 5 + 2 * r))
                            parts.append((kt_ap(1, s, 512 + r, 1024, 4), 128, toep[:, h, 128:256], 5 + 2 * r + 1))
                            unit(g, s, parts, colsel(qh, r, 4, 128), 128, colsel(numb[:, :], r, 4, 128),
                                 colsel(denb[:, :], r, 4, 128), first)
                            first = False
                    else:
                        nk = 32 * (ti + 1)
                        for r in range(16):
                            parts = [(kt_ap(2, s, r, r + 16 * nk, 16), nk, toep[0:nk, h, 128 + 32 * ti:128 + 32 * ti + 32], r)]
                            unit(g, s, parts, colsel(qh, r, 16, 32), 32, colsel(numb[:, :], r, 16, 32),
                                 colsel(denb[:, :], r, 16, 32), first)
                            first = False
                    flush()
                    act(ybuf.ap(h * 512, h * 512 + 512), numb[:, :], AF.Copy, [PS(s)], ybuf.keys(h * 512, h * 512 + 512))
            for s in range(2):
                dve_recip(rden.ap(s * 512, s * 512 + 512), psb[DEN[s]][:, :], [PS(DEN[s])], rden.keys(s * 512, s * 512 + 512))
            for h in range(6):
                s = h % 2
                dve_tt(ybuf.ap(h * 512, h * 512 + 512), ybuf.ap(h * 512, h * 512 + 512), rden.ap(s * 512, s * 512 + 512),
                       ALU.mult, ybuf.keys(h * 512, h * 512 + 512) + rden.keys(s * 512, s * 512 + 512),
                       ybuf.keys(h * 512, h * 512 + 512))

        def sample_attention(l, qT, ybuf, sc, kt_ap, vsn):
            tmpn = sc.get(384, F32)
            pnew = sc.get(384, BF16)
            tmpc = [sc.get(32, F32), sc.get(32, F32)]
            pc = [sc.get(32, BF16), sc.get(32, BF16)]
            rden = sc.get(128, F32)
            NUM, DEN = 0, 1
            SCL = 128.0 ** -0.5
            qh = lambda h: qT.ap(h * 512, h * 512 + 64)
            for h in range(6):
                g, s = h // 2, h % 2
                mm(psb[6][0:64, h * 64:(h + 1) * 64], kt_ap(g, s, 0, 64), qh(h), True, True, ["ktnew"] + qT.keys(), [PS(6)])
            dve_stt(tmpn.ap(p1=64), psb[6][0:64, 0:384], SCL, bnew[:, :, :].rearrange("p h c -> p (h c)"), ALU.mult, ALU.add,
                    [PS(6), "bnew"], tmpn.keys())
            act(pnew.ap(p1=64), tmpn.ap(p1=64), AF.Exp, tmpn.keys(), pnew.keys())
            for h in range(6):
                mm(psb[NUM][:, h * 64:(h + 1) * 64], vsn.ap(h * 128, (h + 1) * 128, p1=64), pnew.ap(h * 64, (h + 1) * 64, p1=64),
                   h == 0, True, vsn.keys() + pnew.keys(), [PS(NUM)])
            for h in range(6):
                s = h % 2
                mm(psb[DEN][:, s * 64:(s + 1) * 64], ones_b[0:64, :], pnew.ap(h * 64, (h + 1) * 64, p1=64), h == 0, True,
                   ["ones_b"] + pnew.keys(), [PS(DEN)])
            def ck(i, si, s):
                o = i * 4608 + (si * 2 + s) * 128
                return kvreg[:, o:o + 128]

            def cv(i, si, s):
                o = i * 4608 + 2304 + (si * 2 + s) * 128
                return kvreg[:, o:o + 128]
            for b in range(NB):
                i = b % 2
                alias = [("kt", l_, g_) for l_ in range(2) for g_ in range(3)] if b < 2 else []
                dma("pool", kvreg[:, i * 4608:i * 4608 + 2304], caK_d[l, b], (), [("cak", i)] + alias)
                dma("pool", kvreg[:, i * 4608 + 2304:i * 4608 + 4608], caV_d[l, b], (), [("cav", i)] + alias)
                pb = 6 + b % 2
                for h in range(6):
                    g, s = h // 2, h % 2
                    if g == 0:
                        mm(psb[pb][:, h * 4:h * 4 + 4], ck(i, 0, s), qh(h)[:, 4 * b:4 * b + 4], True, True,
                           [("cak", i)] + qT.keys(), [PS(pb)])
                    else:
                        for t in range(4):
                            mm(psb[pb][:, h * 4 + t:h * 4 + t + 1], ck(i, 1 + 4 * (g - 1) + t, s), qh(h)[:, 4 * b + t:4 * b + t + 1],
                               True, True, [("cak", i)] + qT.keys(), [PS(pb)])
                dve_stt(tmpc[i].ap(0, 24), psb[pb][:, 0:24], SCL, bc0[:, 0, 0:24], ALU.mult, ALU.add, [PS(pb), "bc0"], tmpc[i].keys())
                act(pc[i].ap(0, 24), tmpc[i].ap(0, 24), AF.Exp, tmpc[i].keys(), pc[i].keys())
                for h in range(6):
                    g, s = h // 2, h % 2
                    c0 = h * 64 + 4 * b
                    if g == 0:
                        mm(psb[NUM][:, c0:c0 + 4], cv(i, 0, s), pc[i].ap(h * 4, h * 4 + 4), False, True,
                           [("cav", i)] + pc[i].keys(), [PS(NUM)])
                    else:
                        for t in range(4):
                            mm(psb[NUM][:, c0 + t:c0 + t + 1], cv(i, 1 + 4 * (g - 1) + t, s), pc[i].ap(h * 4 + t, h * 4 + t + 1),
                               False, True, [("cav", i)] + pc[i].keys(), [PS(NUM)])
                    mm(psb[DEN][:, s * 64 + 4 * b:s * 64 + 4 * b + 4], ones_b[:, :], pc[i].ap(h * 4, h * 4 + 4), False, True,
                       ["ones_b"] + pc[i].keys(), [PS(DEN)])
            dve_recip(rden.ap(), psb[DEN][:, 0:128], [PS(DEN)], rden.keys())
            for h in range(6):
                s = h % 2
                dve_tt(ybuf.ap(h * 512, h * 512 + 64), psb[NUM][:, h * 64:(h + 1) * 64], rden.ap(s * 64, s * 64 + 64), ALU.mult,
                       [PS(NUM)] + rden.keys(), ybuf.keys(h * 512, h * 512 + 512))

        def seg_evac_mul(dst_fn, ps_ap_fn, other_fn, segs, ncol, reads, writes):
            pass

        def branch_B(l, ncol, sample, fm_chunks, ybuf, sc, scy, nextbank):
            vn_tok = scy.get(6 * 512, BF16)
            vnT = sc.get(6 * 512, BF16)
            mean = sc.get(512, F32)
            rstd = sc.get(512, F32)
            tmp = [sc.get(512, F32), sc.get(512, F32)]
            S1, S2 = 4, 5
            ntb = (ncol + 127) // 128
            tbw = min(ncol, 128)
            yk = lambda c: ybuf.keys(c * 512, c * 512 + 512)
            ya = lambda c, p0=0, p1=128: ybuf.ap(c * 512, c * 512 + ncol, p0, p1)

            def vh(c, b):
                act(ya(c), psb[b][:, 0:ncol], AF.Gelu_apprx_tanh, [PS(b)], yk(c))
                mm(psb[S1][:, 0:ncol], ones_f[:, :], ya(c), c == 0, c == 5, yk(c) + ["ones_f"], [PS(S1)])
                t = tmp[c % 2]
                act(t.ap(0, ncol), ya(c), AF.Square, yk(c), t.keys())
                mm(psb[S2][:, 0:ncol], ones_f[:, :], t.ap(0, ncol), c == 0, c == 5, t.keys() + ["ones_f"], [PS(S2)])
            for j in range(3):
                fm_chunks(15 + j, vh, 2 * j)
            m_, r_, t0, t1 = mean.ap(0, ncol), rstd.ap(0, ncol), tmp[0].ap(0, ncol), tmp[1].ap(0, ncol)
            dve_ts(m_, psb[S1][:, 0:ncol], 1.0 / 768, None, ALU.mult, ALU.bypass, [PS(S1)], mean.keys())
            dve_tt(t0, m_, m_, ALU.mult, mean.keys(), tmp[0].keys())
            dve_stt(t1, psb[S2][:, 0:ncol], 1.0 / 768, t0, ALU.mult, ALU.subtract, [PS(S2)] + tmp[0].keys(), tmp[1].keys())
            act(t1, t1, AF.Sqrt, tmp[1].keys(), tmp[1].keys(), bias=EPS_AP(), scale=1.0)
            dve_recip(r_, t1, tmp[1].keys(), rstd.keys())
            for c in range(6):
                t = tmp[c % 2]
                dve_tt(t.ap(0, ncol), ya(c), m_, ALU.subtract, yk(c) + mean.keys(), t.keys())
                if sample:
                    dve_stt(ya(c), t.ap(0, ncol), prm[:, l, P_GV + c:P_GV + c + 1], r_, ALU.mult, ALU.mult,
                            t.keys() + rstd.keys() + ["prm"], yk(c))
                    dma("sp", osv[l, :, c, :], ya(c), yk(c), [], is_out=True)
                    dve_copy(vnT.ap(c * 512, c * 512 + ncol), ya(c), yk(c), vnT.keys(c * 512, c * 512 + 512))
                else:
                    dve_stt(vnT.ap(c * 512, c * 512 + ncol), t.ap(0, ncol), prm[:, l, P_GV + c:P_GV + c + 1], r_,
                            ALU.mult, ALU.mult, t.keys() + rstd.keys() + ["prm"], vnT.keys(c * 512, c * 512 + 512))
            for tb in range(ntb):
                pb = 6 + tb % 2
                ptv = psb[pb][:].bitcast(BF16)
                for c in range(6):
                    tr(ptv[0:tbw, c * 128:(c + 1) * 128], vnT.ap(c * 512 + tb * 128, c * 512 + tb * 128 + tbw), ident[:, :],
                       vnT.keys(c * 512, c * 512 + 512) + ["ident"], [PS(pb)])
                act(vn_tok.ap(tb * 768, tb * 768 + 768, p1=tbw), ptv[0:tbw, 0:768], AF.Copy, [PS(pb)],
                    vn_tok.keys(tb * 768, tb * 768 + 768))
            for c in range(6):
                b = nextbank()
                for (p0, p1, g, lo, hi) in SEGS[c]:
                    first = True
                    for tb in range(ntb):
                        if sample:
                            rhs_w, rhs_b = wbig[0:tbw, l, g, :], bsrow[0:1, l, g, :]
                            rk = ["wbig"]
                        else:
                            rhs_w, rhs_b = wsT[:, l, g, :], bsb[0:1, l, g, :]
                            rk = ["wsT", "bsb"]
                        f0 = g * 192 + lo
                        mm(psb[b][p0:p1, tb * 128:tb * 128 + tbw], vn_tok.ap(tb * 768 + f0, tb * 768 + f0 + (hi - lo), p1=tbw),
                           rhs_w, first, False, vn_tok.keys(tb * 768, tb * 768 + 768) + rk, [PS(b)])
                        mm(psb[b][p0:p1, tb * 128:tb * 128 + tbw], ones_b[0:1, 0:p1 - p0], rhs_b, False, True,
                           ["ones_b"] + rk, [PS(b)])
                        first = False
                act(ya(c), psb[b][:, 0:ncol], AF.Copy, [PS(b)], yk(c))
            def uh(c, b):
                t = tmp[c % 2]
                act(t.ap(0, ncol), psb[b][:, 0:ncol], AF.Gelu_apprx_tanh, [PS(b)], t.keys())
                dve_tt(ya(c), ya(c), t.ap(0, ncol), ALU.mult, yk(c) + t.keys(), yk(c))
            for j in range(3):
                fm_chunks(12 + j, uh, 2 * j)

        def branch_C(l, ti, ncol, sample, fm_chunks, ybuf, sc, scy, nextbank):
            dT = scy.get(6 * 512, BF16)
            nb_ = NB if sample else 1
            tpb = ncol // nb_
            W = tpb + 15
            ext = [sc.get(1024, F32), sc.get(1024, F32)]
            sa_ = sc.get(1024, F32)
            sb_ = sc.get(1024, F32)
            LEV = {0: 1, 1: 2, 2: 2, 3: 3, 4: 4, 5: 4}
            LOWLEV = {1: 1, 4: 3}
            WIN = (2, 4, 8, 16)

            def v3(reg, lo=0, hi=None, p0=0, p1=128):
                hi = W if hi is None else hi
                return reg.ap(0, nb_ * W, p0, p1).rearrange("p (b w) -> p b w", w=W)[:, :, lo:hi]

            def ch(c, b):
                e = ext[c % 2]
                pv = psb[b][:, 0:ncol].rearrange("p (b t) -> p b t", t=tpb)
                if sample:
                    pgap = xT[:, 10 + c, 64:304].rearrange("p (b r) -> p b r", r=15)
                    dma("sp", xT[:, 10 + c, 64:304], spT_d[l, :, c, :, :].rearrange("p b r -> p (b r)"), (), [("x", 10 + c)])
                    dve_copy(v3(e, 0, 15), pgap, [("x", 10 + c)], e.keys())
                else:
                    dve_copy(v3(e, 0, 15), ppref[:, l, c:c + 1, :], [("ppref", l, c)], e.keys())
                act(v3(e, 15, W), pv, AF.Copy, [PS(b)], e.keys())
                if sample:
                    dve_copy(pgap, v3(e, tpb, W), e.keys(), [("x", 10 + c)])
                    dma("sp", ospool[l, :, c, :, :].rearrange("p b r -> p (b r)"), xT[:, 10 + c, 64:304], [("x", 10 + c)], [], is_out=True)
                else:
                    dve_copy(ppref[:, l, c:c + 1, :], v3(e, W - 15, W), e.keys(), [("ppref", l, c)])
                    if ti == NT - 1:
                        dma("sp", opool[l, :, c, :], e.ap(W - 15, W), e.keys(), [], is_out=True)
                cur, nxt, sh = e, sa_, 1
                for lev in range(1, LEV[c] + 1):
                    p0 = 64 if (c in LOWLEV and lev > LOWLEV[c]) else 0
                    dve_tt(v3(nxt, sh * 2 - 1, W, p0), v3(cur, sh * 2 - 1, W, p0), v3(cur, sh - 1, W - sh, p0), ALU.add,
                           cur.keys() + nxt.keys(), nxt.keys())
                    if c in LOWLEV and lev == LOWLEV[c]:
                        low = nxt
                    cur, nxt = nxt, (sb_ if nxt is sa_ else sa_)
                    sh *= 2
                for (p0, p1, g, lo, hi) in SEGS[c]:
                    src = low if (c in LOWLEV and p0 == 0) else cur
                    w = WIN[g]
                    if not sample and ti == 0:
                        fc = cs[p0:p1, C_FAC + 16 * g:C_FAC + 16 * g + 16]
                        dve_tt(src.ap(15, 31, p0, p1), src.ap(15, 31, p0, p1), fc, ALU.mult, src.keys() + ["cs"], src.keys())
                    dve_stt(dT.ap(c * 512, c * 512 + ncol, p0, p1).rearrange("p (b t) -> p b t", t=tpb),
                            v3(src, 15, W, p0, p1), 1.0 / w, v3(e, 15, W, p0, p1), ALU.mult, ALU.subtract,
                            src.keys() + e.keys(), dT.keys(c * 512, c * 512 + 512))
            for j in range(3):
                fm_chunks(21 + j, ch, 2 * j)
            for m in range(6):
                b = nextbank()
                for (p0, p1, g, lo, hi) in SEGS[m]:
                    ks = sorted(GSEGS[g], key=lambda t_: -(t_[2] - t_[1]))
                    for n_, (cc, kp0, kp1) in enumerate(ks):
                        mm(psb[b][p0:p1, 0:ncol], wcb[kp0:kp1, l, cc, lo:hi], dT.ap(cc * 512, cc * 512 + ncol, kp0, kp1),
                           n_ == 0, n_ == 1, ["wcb"] + dT.keys(cc * 512, cc * 512 + 512), [PS(b)])
                act(ybuf.ap(m * 512, m * 512 + ncol), psb[b][:, 0:ncol], AF.Identity, [PS(b), "prm"],
                    ybuf.keys(m * 512, m * 512 + 512), scale=prm[:, l, P_CS + m:P_CS + m + 1])

        def branch_M(l, ti, ncol, sample, fm_chunks, ybuf, sc, scy, nextbank):
            qm = scy.get(6 * 512, BF16)
            pt = [sc.get(512, BF16), sc.get(512, BF16)]
            rden = sc.get(512, F32)
            def qh(c, b):
                act(qm.ap(c * 512, c * 512 + ncol), psb[b][:, 0:ncol], AF.Copy, [PS(b)], qm.keys(c * 512, c * 512 + 512))
            for j in range(3):
                fm_chunks(27 + j, qh, 2 * j)
            if sample:
                sample_memattn(l, qm, ybuf, sc)
                return
            dma("sp", memkv[:, :], mkscr[l], [("mkscr", l)], ["memkv"])
            SC = 192.0 ** -0.5
            for h in range(4):
                for mt in range(2):
                    for n_, (cc, p0, p1) in enumerate(GSEGS[h]):
                        mm(psb[4 + mt][:, 0:ncol], memkv[p0:p1, cc * 256 + mt * 128:cc * 256 + mt * 128 + 128],
                           qm.ap(cc * 512, cc * 512 + ncol, p0, p1), n_ == 0, n_ == 1,
                           ["memkv"] + qm.keys(cc * 512, cc * 512 + 512), [PS(4 + mt)])
                    act(pt[mt].ap(0, ncol), psb[4 + mt][:, 0:ncol], AF.Exp, [PS(4 + mt)], pt[mt].keys(), scale=SC)
                for mt in range(2):
                    mm(psb[6][:, 0:ncol], ones_b[:, :], pt[mt].ap(0, ncol), mt == 0, mt == 1, ["ones_b"] + pt[mt].keys(), [PS(6)])
                dve_recip(rden.ap(0, ncol), psb[6][:, 0:ncol], [PS(6)], rden.keys())
                for (cc, p0, p1) in GSEGS[h]:
                    b = nextbank()
                    f0 = 1536 + cc * 128 + p0
                    for mt in range(2):
                        mm(psb[b][p0:p1, 0:ncol], memkv[:, f0 + mt * 768:f0 + mt * 768 + (p1 - p0)], pt[mt].ap(0, ncol),
                           mt == 0, mt == 1, ["memkv"] + pt[mt].keys(), [PS(b)])
                    dve_tt(ybuf.ap(cc * 512, cc * 512 + ncol, p0, p1), psb[b][p0:p1, 0:ncol], rden.ap(0, ncol, p0, p1), ALU.mult,
                           [PS(b)] + rden.keys(), ybuf.keys(cc * 512, cc * 512 + 512))

        def sample_memattn(l, qm, ybuf, sc):
            pm = [sc.get(32, BF16), sc.get(32, BF16)]
            rden = sc.get(256, F32)
            NUM, DEN = 2, 3
            SC = 192.0 ** -0.5
            MB = 9216
            first = True
            for b in range(NB):
                i = b % 2
                ko, vo = MB + i * 3072, MB + i * 3072 + 1536
                alias = [("kt", l_, g_) for l_ in range(2) for g_ in range(3)] if b < 2 else []
                dma("pool", kvreg[:, ko:ko + 1536], cmK_d[l, b], (), [("cmk", i)] + alias)
                dma("pool", kvreg[:, vo:vo + 1536], cmV_d[l, b], (), [("cmv", i)] + alias)
                for mt in range(2):
                    for h in range(4):
                        c0 = (mt * 4 + h) * 4
                        pb = 4 + 2 * (b % 2) + (h % 2)
                        for n_, (cc, p0, p1) in enumerate(GSEGS[h]):
                            mm(psb[pb][:, c0:c0 + 4], kvreg[p0:p1, ko + cc * 256 + mt * 128:ko + cc * 256 + mt * 128 + 128],
                               qm.ap(cc * 512 + 4 * b, cc * 512 + 4 * b + 4, p0, p1), n_ == 0, n_ == 1,
                               [("cmk", i)] + qm.keys(cc * 512, cc * 512 + 512), [PS(pb)])
                for hp in range(2):
                    pb = 4 + 2 * (b % 2) + hp
                    pv_ = pm[i].ap(0, 32).rearrange("p (m h q) -> p m h q", m=2, h=4)
                    sv_ = psb[pb][:, 0:32].rearrange("p (m h q) -> p m h q", m=2, h=4)
                    for hh in (hp, hp + 2):
                        act(pv_[:, :, hh, :], sv_[:, :, hh, :], AF.Exp, [PS(pb)], pm[i].keys(), scale=SC)
                if stage == 14:
                    continue
                for h in range(4):
                    for (cc, p0, p1) in GSEGS[h]:
                        for mt in range(2):
                            c0 = (mt * 4 + h) * 4
                            f0 = vo + mt * 768 + cc * 128 + p0
                            mm(psb[NUM][p0:p1, cc * 64 + 4 * b:cc * 64 + 4 * b + 4], kvreg[:, f0:f0 + (p1 - p0)],
                               pm[i].ap(c0, c0 + 4), first, True, [("cmv", i)] + pm[i].keys(), [PS(NUM)])
                            first = False
                    if stage == 15:
                        continue
                    for mt in range(2):
                        c0 = (mt * 4 + h) * 4
                        mm(psb[DEN][:, h * 64 + 4 * b:h * 64 + 4 * b + 4], ones_b[:, :], pm[i].ap(c0, c0 + 4),
                           b == 0 and h == 0 and mt == 0, True, ["ones_b"] + pm[i].keys(), [PS(DEN)])
            if stage in (14, 15):
                return
            dve_recip(rden.ap(), psb[DEN][:, 0:256], [PS(DEN)], rden.keys())
            for h in range(4):
                for (cc, p0, p1) in GSEGS[h]:
                    dve_tt(ybuf.ap(cc * 512, cc * 512 + 64, p0, p1), psb[NUM][p0:p1, cc * 64:cc * 64 + 64],
                           rden.ap(h * 64, h * 64 + 64, p0, p1), ALU.mult, [PS(NUM)] + rden.keys(), ybuf.keys(cc * 512, cc * 512 + 512))

        def prompt_memkv():
            for c in range(KC):
                dma("sp", xT[:, c, 0:256], mpT[:, c, :], (), [("x", c)])
            for l in range(2):
                norm_to_h(l, P_MEM, 256, A_SCR)
                if CUT < 5:
                    continue
                sc = Alloc(A_SCR + 8, A_END)
                st = [sc.get(256, F32), sc.get(256, F32)]
                kb = [sc.get(256, BF16), sc.get(256, BF16)]
                n = 0
                for j in range(6):
                    slot = wload(wmk[l][:, j * 4096:(j + 1) * 4096], 4096)
                    wv = wview(slot, KC, 256)
                    for mt in range(2):
                        b = n % 4
                        for k in range(KC):
                            mm(psb[b][:, 0:256], hT(k, 256)[:, mt * 128:(mt + 1) * 128], wv[:, k, :], k == 0, k == KC - 1,
                               [("w", slot)] + hTk(k), [PS(b)])
                        s_, k_ = st[n % 2], kb[n % 2]
                        act(s_.ap(), psb[b][:, 0:256], AF.Copy, [PS(b)], s_.keys())
                        dma("sp", omkv[l, mt * 128:(mt + 1) * 128, j * 256:(j + 1) * 256], s_.ap(), s_.keys(), [], is_out=True)
                        if CUT < 6:
                            n += 1
                            continue
                        if CUT == 76:
                            if j < 3:
                                o = (n % 2) * 256
                                dve_copy(memkv[:, o:o + 256], s_.ap(), s_.keys(), [("memkv", n % 2)])
                            n += 1
                            continue
                        if CUT == 74:
                            if j < 3:
                                o = (n % 2) * 256
                                dve_copy(memkv[:, o:o + 256], psb[b][:, 0:256], [PS(b)], [("memkv", n % 2)])
                            n += 1
                            continue
                        if CUT == 75:
                            if j >= 3:
                                k_ = kb[n % 2]
                                dve_copy(k_.ap(), psb[b][:, 0:256], [PS(b)], k_.keys())
                            n += 1
                            continue
                        if CUT in (72, 73):
                            if CUT == 72 and j >= 3:
                                o = 1536 + mt * 768 + (j - 3) * 256
                                dve_copy(memkv[:, o:o + 256], psb[b][:, 0:256], [PS(b)], ["memkv"])
                            n += 1
                            continue
                        if j < 3:
                            if globals().get("KCOPY_ACT", False):
                                act(k_.ap(), psb[b][:, 0:256], AF.Copy, [PS(b)], k_.keys())
                            else:
                                dve_copy(k_.ap(), psb[b][:, 0:256], [PS(b)], k_.keys())
                            if CUT in (61, 71):
                                n += 1
                                continue
                            pb = 4 + n % 2
                            ptb_ = psb[pb][:].bitcast(BF16)
                            for q in range(2):
                                tr(ptb_[:, q * 128:(q + 1) * 128], k_.ap(q * 128, q * 128 + 128), ident[:, :],
                                   k_.keys() + ["ident"], [PS(pb)])
                                cc = 2 * j + q
                                if CUT == 62:
                                    continue
                                act(memkv[:, cc * 256 + mt * 128:cc * 256 + mt * 128 + 128], ptb_[:, q * 128:(q + 1) * 128],
                                    AF.Copy, [PS(pb)], ["memkv"])
                        elif CUT not in (63, 71):
                            o = 1536 + mt * 768 + (j - 3) * 256
                            dve_copy(memkv[:, o:o + 256], psb[b][:, 0:256], [PS(b)], ["memkv"])
                        n += 1
                if CUT >= 7 and CUT not in (71, 72):
                    dma("sp", mkscr[l], memkv[:, :], ["memkv"], [("mkscr", l)])


        if CUT >= 4:
            prompt_memkv()
        for ti in range(globals().get("NT_RUN", NT)):
            s0 = ti * T
            for c in range(KC):
                dma("sp", xT[:, c, :], xpT[:, c, s0:s0 + T], (), [("x", c)])
            for l in range(2):
                layer(l, ti, T, False)
            for c in range(KC):
                dma("sp", ypT[:, c, s0:s0 + T], xT[:, c, :], [("x", c)], [], is_out=True)

        if globals().get("RUN_SAMPLE", True):
            sa2 = Alloc(A_SCR, A_END)
            r_wb = sa2.get(2 * 4 * 64, F32)
            r_br = sa2.get(2 * 4 * 64, F32)
            dma("sp", r_wb.ap(0, 512, 0, 64), wrep_d.rearrange("p l g c -> p (l g c)"), (), r_wb.keys())
            dma("sp", r_br.ap(0, 512, 0, 1), brep_d.rearrange("p l g c -> p (l g c)"), (), r_br.keys())
            for l in range(2):
                for g in range(4):
                    dve_tt(wbig[:, l, g, :], r_wb.ap(l * 256 + g * 64, l * 256 + g * 64 + 64, 0, 64), cs[0:64, C_MB:C_MB + 64], ALU.mult,
                           r_wb.keys() + ["cs"], ["wbig"])
            dve_copy(bsrow[:, :, :, :].rearrange("p l g c -> p (l g c)"), r_br.ap(0, 512, 0, 1), r_br.keys(), ["wbig"])
            dma("sp", prm[:, :, :], par2[:, :, :], (), ["prm"])
            for c in range(KC):
                dma("sp", xT[:, c, 0:TS], xsT[:, c, :], (), [("x", c)])
            for l in range(2):
                layer(l, 0, TS, True)
            for c in range(KC):
                dma("sp", ysT[:, c, :], xT[:, c, 0:TS], [("x", c)], [], is_out=True)

        S.finish(nc, sem, dsem)
        with nc.Block() as block:
            @block.sync
            def _(e):
                S.emit("sp", e)

            @block.gpsimd
            def _(e):
                S.emit("pool", e)

            @block.tensor
            def _(e):
                S.emit("pe", e)

            @block.scalar
            def _(e):
                S.emit("act", e)

            @block.vector
            def _(e):
                S.emit("dve", e)
    return nc


def _rel_bucket(dist):
    exact = 16
    d = np.maximum(dist.astype(np.float32), 1.0)
    log_b = exact + (np.log(d / exact) / np.float32(np.log(2048 / exact)) * (32 - exact)).astype(np.int32)
    return np.where(dist < exact, dist, np.minimum(log_b, 31))


def _bucket_tables():
    out = []
    for d in A_DIL:
        j = np.arange(129)
        out.append(sorted(set(int(b) for b in _rel_bucket(j * d))))
    return out


W_IN_BLOCK_COLS = ([768 + 256 * g for g in range(3)] + [1536 + 256 * g for g in range(3)] +
                   [0, 256, 512] + [COL_G + 256 * j for j in range(3)] +
                   [COL_B + 256 * j for j in range(6)] + [COL_G + 768 + 256 * j for j in range(3)] +
                   [COL_C + 256 * j for j in range(3)] + [COL_G + 1536 + 256 * j for j in range(3)] +
                   [COL_M + 256 * j for j in range(3)] + [COL_G + 2304 + 256 * j for j in range(3)])


def _blk(w, nk):
    n = w.shape[1]
    return np.ascontiguousarray(w.reshape(nk, 128, n).transpose(1, 0, 2)).reshape(128, nk * n)


def _pack_layer(w_in, w_out, w_up, w_down):
    parts = [_blk(w_in[:, c0:c0 + 256], 16) for c0 in W_IN_BLOCK_COLS]
    parts += [_blk(w_out[:, m * 128:(m + 1) * 128], 24) for m in range(16)]
    for j in range(NPAIR):
        parts.append(_blk(np.concatenate([w_up[:, j * 128:(j + 1) * 128],
                                          w_up[:, DFF + j * 128:DFF + (j + 1) * 128]], axis=1), 16))
    for m in range(16):
        parts.append(_blk(w_down[0:22 * 128, m * 128:(m + 1) * 128], 22))
        parts.append(_blk(w_down[22 * 128:43 * 128, m * 128:(m + 1) * 128], 21))
    out = np.concatenate(parts, axis=1)
    assert out.shape == (128, 448512), out.shape
    return out


def _fm(v, nchunk):
    return np.ascontiguousarray(v.reshape(nchunk, 128).T)


def _consts():
    c = np.zeros((128, 1536), np.float32)
    c[:, 0:128] = np.eye(128, dtype=np.float32)
    c[:, 128:256] = 1.0
    jj, ii = np.meshgrid(np.arange(128), np.arange(128), indexing="ij")
    c[:, 256:384] = (ii >= jj).astype(np.float32)
    k = np.arange(128)[:, None]
    col = np.arange(256)[None, :]
    dist_idx = np.where(col < 128, col - k + 128, col - 128 - k)
    valid = (dist_idx >= 0) & (dist_idx <= 128)
    for g, d in enumerate(A_DIL):
        bk = _rel_bucket(np.clip(dist_idx, 0, 128) * d).astype(np.float32)
        c[:, 384 + 256 * g:384 + 256 * (g + 1)] = np.where(valid, bk, -1.0)
    for g, w in enumerate((2, 4, 8, 16)):
        t = np.arange(16)
        c[:, 1152 + 16 * g:1152 + 16 * (g + 1)] = (w / np.minimum(t + 1, w)).astype(np.float32)[None, :]
    r = np.arange(64)
    same_b = (r[:, None] // 4) == (r[None, :] // 4)
    causal = (r[:, None] % 4) <= (r[None, :] % 4)
    c[0:64, 1216:1280] = (same_b & causal).astype(np.float32)
    c[0:64, 1280:1344] = np.where(same_b & causal, 0.0, NEG)
    c[0:64, 1344:1408] = np.where(r[:, None] == r[None, :], 0.0, NEG)
    return c


_CACHE = {}


PROMPT_CORE_SEQ = {0: 0, 1: 1, 4: 2, 5: 3}
SEQ_CORE = [0, 1, 4, 5]


def _prepare(inp, cores=range(8)):
    f = lambda k: np.asarray(inp[k], dtype=np.float32)
    w_in, w_out, w_up, w_down, w_mkv = f("w_in"), f("w_out"), f("w_up"), f("w_down"), f("w_mem_kv")
    wst = np.stack([_pack_layer(w_in[l], w_out[l], w_up[l], w_down[l]) for l in range(2)])
    wmk = np.stack([np.concatenate([_blk(w_mkv[l][:, j * 256:(j + 1) * 256], 16) for j in range(6)], axis=1)
                    for l in range(2)])
    par = np.zeros((128, 2, NPAR), np.float32)
    for l in range(2):
        for off, key in ((P_PRE, "norm_pre_mix"), (P_MEM, "norm_mem"), (P_POSTM, "norm_post_mix"),
                         (P_PREF, "norm_pre_ffn"), (P_POSTF, "norm_post_ffn")):
            par[:, l, off:off + 16] = _fm(f(key)[l], 16)
        par[:, l, P_GV:P_GV + 6] = _fm(f("norm_v_b")[l], 6)
        par[:, l, P_CS:P_CS + 6] = _fm(f("pool_scale")[l], 6)
        for t in range(3):
            par[:, l, P_CW + 86 * t:P_CW + 86 * (t + 1)] = _fm(f("conv_w")[l, t], 86)
        par[:, l, P_CB:P_CB + 86] = _fm(f("conv_b")[l], 86)
    wsT = np.ascontiguousarray(f("w_spatial").transpose(0, 3, 1, 2))
    bsr = np.ascontiguousarray(f("b_spatial")[:, None, :, :])
    wcp = np.ascontiguousarray(f("w_pool").reshape(2, 6, 128, 192).transpose(0, 2, 1, 3))
    cst = _consts()
    ws_, bs__ = f("w_spatial"), f("b_spatial")
    wrep = np.ascontiguousarray(np.broadcast_to(ws_[:, :, 0:4, 0:4].transpose(3, 0, 1, 2)[None, :, :, :, None, :],
                                                (NB, 4, 2, 4, NB, 4))).reshape(64, 2, 4, 64)
    brep = np.ascontiguousarray(np.broadcast_to(bs__[:, :, None, 0:4], (2, 4, NB, 4))).reshape(1, 2, 4, 64)
    xp, xs, mp = f("x_prompt"), f("x_sample"), f("mem_prompt")
    ca = [f("cache_a_w128"), f("cache_a_w512"), f("cache_a_w2048")]
    cmem, spool, sconv = f("cache_mem_kv"), f("state_pool"), f("state_conv")

    def fmT(x):
        return np.ascontiguousarray(x.T.reshape(16, 128, x.shape[0]).transpose(1, 0, 2))

    in_maps = []
    for c in cores:
        bi = PROMPT_CORE_SEQ.get(c, -1)
        bs_ = slice(NB * c, NB * (c + 1))
        rows = [np.arange(128)] + [t + 4 * np.arange(128) for t in range(4)] + [t + 16 * np.arange(128) for t in range(4)]
        src = [0, 1, 1, 1, 1, 2, 2, 2, 2]
        caK = np.zeros((2, NB, 128, 9, 2, 128), np.float32)
        caV = np.zeros((2, NB, 128, 9, 2, 128), np.float32)
        for si in range(9):
            blk = ca[src[si]][:, bs_][:, :, rows[si]]
            caK[:, :, :, si] = blk[:, :, :, 0].transpose(0, 1, 4, 3, 2)
            caV[:, :, :, si] = blk[:, :, :, 1]
        cm = cmem[:, bs_].reshape(2, NB, 256, 2, 768)
        cmK = np.ascontiguousarray(cm[:, :, :, 0].reshape(2, NB, 256, 6, 128).transpose(0, 1, 4, 3, 2))
        cmV = np.ascontiguousarray(cm[:, :, :, 1].reshape(2, NB, 2, 128, 768).transpose(0, 1, 3, 2, 4))
        spT = np.ascontiguousarray(spool[:, bs_].reshape(2, NB, 15, 6, 128).transpose(0, 4, 3, 1, 2))
        scT = np.ascontiguousarray(sconv[:, bs_].reshape(2, NB, 2, 86, 128).transpose(0, 4, 3, 1, 2))
        in_maps.append({
            "xpT": fmT(xp[bi]) if bi >= 0 else np.zeros((128, KC, SEQ), np.float32),
            "xsT": fmT(xs[bs_].reshape(TS, D)),
            "mpT": fmT(mp[bi]) if bi >= 0 else np.zeros((128, KC, 256), np.float32),
            "wst": wst, "wmk": wmk, "par": par if bi >= 0 else np.zeros_like(par), "par2": par, "wsT": wsT, "bsr": bsr, "wcp": wcp,
            "relb": f("rel_bias"), "cst": cst,
            "caK": caK.reshape(2, NB, 128, 2304), "caV": caV.reshape(2, NB, 128, 2304),
            "cmK": cmK.reshape(2, NB, 128, 1536), "cmV": cmV.reshape(2, NB, 128, 1536),
            "spT": spT, "scT": scT, "wrep": wrep, "brep": brep,
        })
    return in_maps


def kernel(**inp):
    nc = build_program(_CACHE.get("stage", 99))
    in_maps = _prepare(inp)
    res = run_bass_kernel_spmd(nc, in_maps, core_ids=list(range(8)))
    return _assemble(res.results)


def _assemble(R):

    def tokmajor(a):
        return np.ascontiguousarray(a.transpose(2, 1, 0)).reshape(a.shape[2], -1)

    y_p = np.stack([tokmajor(R[SEQ_CORE[b]]["ypT"]) for b in range(4)])
    y_s = np.concatenate([tokmajor(R[c]["ysT"]).reshape(NB, 4, D) for c in range(8)], axis=0)
    a_p = [np.stack([R[SEQ_CORE[b]]["oa%d" % g] for b in range(4)], axis=1).reshape(2, 4, -1, 2, 2, 128) for g in range(3)]
    mkv_p = np.stack([R[SEQ_CORE[b]]["omkv"] for b in range(4)], axis=1).reshape(2, 4, 256, 2, 4, 192)
    pool_p = np.stack([R[SEQ_CORE[b]]["opool"].transpose(0, 3, 2, 1).reshape(2, 15, 768) for b in range(4)], axis=1)
    conv_p = np.stack([R[SEQ_CORE[b]]["oconv"].transpose(0, 3, 2, 1).reshape(2, 2, 2 * DFF) for b in range(4)], axis=1)
    a_s = [np.concatenate([R[c]["osa%d" % g].reshape(2, NB, 4, 2, 2, 128) for c in range(8)], axis=1) for g in range(3)]
    v_s = np.concatenate([R[c]["osv"].transpose(0, 3, 2, 1).reshape(2, NB, 4, 768) for c in range(8)], axis=1)
    pool_s = np.concatenate([R[c]["ospool"].transpose(0, 3, 4, 2, 1).reshape(2, NB, 15, 768) for c in range(8)], axis=1)
    conv_s = np.concatenate([R[c]["osconv"].transpose(0, 3, 4, 2, 1).reshape(2, NB, 2, 2 * DFF) for c in range(8)], axis=1)
    outs = (y_p, y_s, a_p[0], a_p[1], a_p[2], mkv_p, pool_p, conv_p, a_s[0], a_s[1], a_s[2], v_s, pool_s, conv_s)
    return tuple(np.ascontiguousarray(o, dtype=np.float32) for o in outs)
```

```python
import numpy as np
import concourse.bass as bass
import concourse.mybir as mybir
from concourse.bass_utils import run_bass_kernel_spmd

F32 = mybir.dt.float32
BF16 = mybir.dt.bfloat16
AF = mybir.ActivationFunctionType
ALU = mybir.AluOpType

D = 2048
KC = 16
T = 512
NT = 4
SEQ = 2048
TS = 64
NB = 16
DFF = 5504
NPAIR = 43
EPS = 1e-6
NEG = -1e30
COL_A, COL_B, COL_C, COL_M, COL_G = 0, 2304, 3840, 4608, 5376
A_DIL = (1, 4, 16)

A_MERGED, A_REST, A_HT, A_SCR, A_END = 0, 24, 43, 59, 82
WSLOTS = 3
WSLOT_ELEMS = 4096

P_PRE, P_MEM, P_POSTM, P_PREF, P_POSTF = 0, 16, 32, 48, 64
P_GV, P_CS, P_CW, P_CB, NPAR = 80, 86, 92, 92 + 258, 92 + 258 + 86

SEGS = {0: [(0, 128, 0, 0, 128)], 1: [(0, 64, 0, 128, 192), (64, 128, 1, 0, 64)],
        2: [(0, 128, 1, 64, 192)], 3: [(0, 128, 2, 0, 128)],
        4: [(0, 64, 2, 128, 192), (64, 128, 3, 0, 64)], 5: [(0, 128, 3, 64, 192)]}
GSEGS = {0: [(0, 0, 128), (1, 0, 64)], 1: [(1, 64, 128), (2, 0, 128)],
         2: [(3, 0, 128), (4, 0, 64)], 3: [(4, 64, 128), (5, 0, 128)]}


class Op:
    __slots__ = ("eng", "fn", "deps", "signal", "sem", "val", "waits", "dma", "idx")


class Sched:
    COMPUTE = ("pe", "act", "dve")
    QUEUES = ("sp", "pool")

    def __init__(self):
        self.ops = []
        self.last_w = {}
        self.readers = {}
        self.final = []

    def add(self, eng, fn, reads=(), writes=(), out=False):
        op = Op()
        op.eng, op.fn, op.signal, op.dma, op.idx = eng, fn, False, eng in self.QUEUES, len(self.ops)
        deps = {}
        psr = [r for r in reads if isinstance(r, tuple) and r[0] == "ps"]
        if psr:
            writes = list(writes) + [r for r in psr if r not in writes]

        def dep(o):
            if o is None:
                return
            if o.dma:
                deps[("d", o.idx)] = o
            else:
                k = ("c", o.eng)
                if k not in deps or deps[k].idx < o.idx:
                    deps[k] = o

        for r in reads:
            dep(self.last_w.get(r))
        for w in writes:
            dep(self.last_w.get(w))
            for o in self.readers.get(w, {}).values():
                dep(o)
        for r in reads:
            rd = self.readers.setdefault(r, {})
            rd[("d", op.idx) if op.dma else ("c", eng)] = op
        for w in writes:
            self.last_w[w] = op
            self.readers[w] = {}
        op.deps = [o for o in deps.values() if not (o.eng == "pe" and eng == "pe")]
        for o in op.deps:
            o.signal = True
        self.ops.append(op)
        if out:
            self.final.append(op)
        return op

    def finish(self, nc, sems, dma_sems):
        fin = Op()
        fin.eng, fin.fn, fin.signal, fin.dma, fin.idx = "sp", None, False, True, len(self.ops)
        fin.deps = list(self.final)
        for o in fin.deps:
            o.signal = True
        self.ops.append(fin)
        cnt = {e: 0 for e in self.COMPUTE}
        dcnt = {q: 0 for q in self.QUEUES}
        dsem_val = {}
        waited = {e: {} for e in self.COMPUTE + self.QUEUES}
        for op in self.ops:
            waits = []

            def need(sem, val, _w=waits, _e=op.eng):
                if waited[_e].get(id(sem), 0) < val:
                    waited[_e][id(sem)] = val
                    _w.append((sem, val))

            for o in op.deps:
                need(o.sem, o.val)
            if op.fn is not None:
                if op.dma:
                    pool = dma_sems[op.eng]
                    s = pool[dcnt[op.eng] % len(pool)]
                    dcnt[op.eng] += 1
                    prev = dsem_val.get(id(s), 0)
                    if prev:
                        need(s, prev)
                    op.sem, op.val = s, prev + 16
                    dsem_val[id(s)] = prev + 16
                    op.signal = True
                elif op.signal:
                    cnt[op.eng] += 1
                    op.sem, op.val = sems[op.eng], cnt[op.eng]
            op.waits = waits

    def emit(self, engname, eng):
        for op in self.ops:
            if op.eng != engname:
                continue
            for sem, val in op.waits:
                eng.wait_ge(sem, val)
            if op.fn is None:
                continue
            ins = op.fn(eng)
            if op.signal:
                ins.then_inc(op.sem, 16 if op.dma else 1)


def _bf16_bits_placeholder():
    return None


def build_program(stage=99):
    nc = bass.Bass("TRN2", target_bir_lowering=False)
    S = Sched()

    def din(name, shape, dt=F32):
        return nc.dram_tensor(name, list(shape), dt, kind="ExternalInput").ap()

    def dout(name, shape, dt=F32):
        return nc.dram_tensor(name, list(shape), dt, kind="ExternalOutput").ap()

    def dscr(name, shape, dt):
        return nc.dram_tensor(name, list(shape), dt, kind="Internal").ap()

    xpT = din("xpT", [128, KC, SEQ])
    xsT = din("xsT", [128, KC, TS])
    mpT = din("mpT", [128, KC, 256])
    SMALL = globals().get("DBG_SMALL", False)
    wst = din("wst", [2, 128, 448512 if not SMALL else 8])
    wmk = din("wmk", [2, 128, 24576])
    par = din("par", [128, 2, NPAR])
    par2 = din("par2", [128, 2, NPAR])
    wsT_d = din("wsT", [2, 128, 4, 128])
    bs_d = din("bsr", [2, 1, 4, 128])
    wc_d = din("wcp", [2, 128, 6, 192])
    relb = din("relb", [32, 6])
    cst = din("cst", [128, 1536])
    caK_d = din("caK", [2, NB, 128, 2304 if not SMALL else 8])
    caV_d = din("caV", [2, NB, 128, 2304 if not SMALL else 8])
    cmK_d = din("cmK", [2, NB, 128, 1536 if not SMALL else 8])
    cmV_d = din("cmV", [2, NB, 128, 1536 if not SMALL else 8])
    spT_d = din("spT", [2, 128, 6, NB, 15])
    scT_d = din("scT", [2, 128, 86, NB, 2])
    wrep_d = din("wrep", [64, 2, 4, 64])
    brep_d = din("brep", [1, 2, 4, 64])
    ypT = dout("ypT", [128, KC, SEQ])
    ysT = dout("ysT", [128, KC, TS])
    oa = [dout("oa0", [2, 128, 512]), dout("oa1", [2, 512, 512]), dout("oa2", [2, 2048, 512])]
    omkv = dout("omkv", [2, 256, 1536])
    opool = dout("opool", [2, 128, 6, 15])
    oconv = dout("oconv", [2, 128, 86, 2])
    osa = [dout("osa%d" % g, [2, TS, 512]) for g in range(3)]
    osv = dout("osv", [2, 128, 6, TS])
    ospool = dout("ospool", [2, 128, 6, NB, 15])
    osconv = dout("osconv", [2, 128, 86, NB, 2])
    vscr = dscr("vscr", [2, SEQ, 768], BF16)
    mkscr = dscr("mkscr", [2, 128, 3072], BF16)
    wscr = dscr("wscr", [2, 128, 448512], BF16)

    import contextlib
    es = contextlib.ExitStack()
    with es:
        def sb(name, shape, dt):
            return es.enter_context(nc.sbuf_tensor(name, list(shape), dt))

        xT = sb("xT", [128, KC, T], F32)
        arena = sb("arena", [128, A_END * 512], BF16)
        kvreg = sb("kvreg", [128, 16384], BF16)
        wbuf = sb("wbuf", [128, WSLOTS, WSLOT_ELEMS], BF16)
        toep = sb("toep", [128, 6, 256], F32)
        memkv = sb("memkv", [128, 3072], BF16)
        prm = sb("prm", [128, 2, NPAR], F32)
        wsT = sb("wsTb", [128, 2, 4, 128], BF16)
        wcb = sb("wcb", [128, 2, 6, 192], BF16)
        bsb = sb("bsb", [1, 2, 4, 128], BF16)
        cs = sb("cs", [128, 1536], F32)
        ident = sb("ident", [128, 128], BF16)
        ones_b = sb("ones_b", [128, 128], BF16)
        ones_f = sb("ones_f", [128, 128], F32)
        ppref = sb("ppref", [128, 2, 6, 15], F32)
        cpref = sb("cpref", [128, 2, 86, 2], F32)
        wbig = sb("wbig", [64, 2, 4, 64], BF16)
        bsrow = sb("bsrow", [1, 2, 4, 64], BF16)
        bnew = sb("bnew", [64, 6, 64], F32)
        bc0 = sb("bc0", [128, 2, 64], F32)
        psb = [es.enter_context(nc.psum_tensor("ps%d" % i, [128, 512], F32)) for i in range(8)]
        sem = {e: es.enter_context(nc.semaphore("s_" + e)) for e in Sched.COMPUTE}
        dsem = {q: [es.enter_context(nc.semaphore("d_%s%d" % (q, i))) for i in range(12)]
                for q in Sched.QUEUES}

        arena_f = arena[:].bitcast(F32)

        def ar_keys(slot0, nslots):
            return [("ar", s) for s in range(int(slot0), int(slot0 + nslots))]

        class Reg:
            def __init__(self, slot0, elems, dt):
                self.slot0, self.elems, self.dt = slot0, elems, dt
                self.bytes = elems * (4 if dt == F32 else 2)
                assert self.bytes % 1024 == 0 or True
                self.nslots = (self.bytes + 1023) // 1024
                assert slot0 + self.nslots <= A_END, (slot0, self.nslots)

            def ap(self, lo=0, hi=None, p0=0, p1=128):
                hi = self.elems if hi is None else hi
                if self.dt == F32:
                    b = self.slot0 * 256
                    return arena_f[p0:p1, b + lo:b + hi]
                b = self.slot0 * 512
                return arena[p0:p1, b + lo:b + hi]

            def keys(self, lo=0, hi=None):
                hi = self.elems if hi is None else hi
                esz = 4 if self.dt == F32 else 2
                s0 = self.slot0 + (lo * esz) // 1024
                s1 = self.slot0 + (hi * esz + 1023) // 1024
                return [("ar", s) for s in range(s0, s1)]

        class Alloc:
            def __init__(self, slot0, slot1):
                self.cur, self.end = slot0, slot1

            def get(self, elems, dt):
                r = Reg(self.cur, elems, dt)
                self.cur += r.nslots
                assert self.cur <= self.end, "arena scratch overflow"
                return r

        PS = lambda b: ("ps", b)

        def mm(out, lhsT, rhs, start, stop, reads, writes):
            S.add("pe", lambda e: e.matmul(out, lhsT=lhsT, rhs=rhs, start=start, stop=stop,
                                           skip_group_check=True), reads, writes)

        def tr(out, in_, idn, reads, writes):
            S.add("pe", lambda e: e.transpose(out, in_, idn), reads, writes)

        def act(out, in_, func, reads, writes, bias=None, scale=None):
            kw = {}
            if bias is not None:
                kw["bias"] = bias
            if scale is not None:
                kw["scale"] = scale
            S.add("act", lambda e: e.activation(out=out, in_=in_, func=func, **kw), reads, writes)

        def dve_copy(out, in_, reads, writes):
            S.add("dve", lambda e: e.tensor_copy(out=out, in_=in_), reads, writes)

        def dve_tt(out, a, b, op, reads, writes):
            S.add("dve", lambda e: e.tensor_tensor(out=out, in0=a, in1=b, op=op), reads, writes)

        def dve_ts(out, a, s1, s2, op0, op1, reads, writes):
            S.add("dve", lambda e: e.tensor_scalar(out=out, in0=a, scalar1=s1, scalar2=s2, op0=op0,
                                                   op1=op1), reads, writes)

        def dve_stt(out, a, s, b, op0, op1, reads, writes):
            S.add("dve", lambda e: e.scalar_tensor_tensor(out=out, in0=a, scalar=s, in1=b, op0=op0,
                                                          op1=op1), reads, writes)

        def dve_recip(out, in_, reads, writes):
            S.add("dve", lambda e: e.reciprocal(out=out, in_=in_), reads, writes)

        def dve_memset(out, val, writes):
            S.add("dve", lambda e: e.memset(out, val), (), writes)

        def dma(q, out, in_, reads, writes, is_out=False):
            S.add(q, lambda e: e.dma_start(out=out, in_=in_), reads, writes, out=is_out)

        wstate = {"n": 0}

        def wload(src_ap, nelem):
            slot = wstate["n"] % WSLOTS
            wstate["n"] += 1
            assert nelem <= WSLOT_ELEMS
            dma("pool", wbuf[:, slot, 0:nelem], src_ap, (), [("w", slot)])
            return slot

        def wview(slot, nk, ncol):
            return wbuf[:, slot, 0:nk * ncol].rearrange("p (k n) -> p k n", k=nk)

        W_IN_OFF = 0
        W_OUT_OFF = 135168
        W_UP_OFF = W_OUT_OFF + 49152
        W_DN_OFF = W_UP_OFF + 176128

        C_ID, C_ONES, C_TRIL, C_BK, C_FAC, C_MB, C_MN = 0, 128, 256, 384, 1152, 1216, 1280
        dma("sp", cs[:, :], cst[:, :], (), ["cs"])
        dma("sp", prm[:, :, :], par[:, :, :], (), ["prm"])
        dve_copy(ident[:, :], cs[:, C_ID:C_ID + 128], ["cs"], ["ident"])
        dve_copy(ones_b[:, :], cs[:, C_ONES:C_ONES + 128], ["cs"], ["ones_b"])
        dve_copy(ones_f[:, :], cs[:, C_ONES:C_ONES + 128], ["cs"], ["ones_f"])
        dve_memset(ppref[:, :, :, :], 0.0, ["ppref"])
        dve_memset(cpref[:, :, :, :], 0.0, ["cpref"])

        CUT = globals().get('SETUP_CUT', 99)
        sa = Alloc(A_SCR, A_END)
        r_ws = sa.get(2 * 512, F32)
        r_wc = sa.get(2 * 1152, F32)
        r_bs = sa.get(1024, F32)
        dma("sp", r_ws.ap().rearrange("p (l g i) -> p l g i", l=2, g=4),
            wsT_d.rearrange("l p g i -> p l g i"), (), r_ws.keys())
        dma("sp", r_wc.ap().rearrange("p (l c d) -> p l c d", l=2, c=6),
            wc_d.rearrange("l p c d -> p l c d"), (), r_wc.keys())
        dma("sp", r_bs.ap(p0=0, p1=1).rearrange("p (l g i) -> p l g i", l=2, g=4),
            bs_d.rearrange("l p g i -> p l g i"), (), r_bs.keys())
        for l in range(2):
            for g in range(4):
                dve_tt(wsT[:, l, g, :], r_ws.ap(l * 512 + g * 128, l * 512 + g * 128 + 128),
                       cs[:, C_TRIL:C_TRIL + 128], ALU.mult, r_ws.keys() + ["cs"], ["wsT"])
        dve_copy(wcb[:, :, :, :], r_wc.ap().rearrange("p (l c d) -> p l c d", l=2, c=6), r_wc.keys(), ["wcb"])
        dve_copy(bsb[:, :, :, :], r_bs.ap(p0=0, p1=1).rearrange("p (l g i) -> p l g i", l=2, g=4),
                 r_bs.keys(), ["bsb"])

        r_rb = sa.get(256, F32)
        dma("sp", r_rb.ap(0, 192), bass.AP(relb.tensor, 0, [[0, 128], [1, 192]]), (), r_rb.keys())
        r_tmp = sa.get(256, F32)
        BUCKETS = _bucket_tables()
        for h in range(6 if CUT >= 2 else 0):
            g = h // 2
            bk = cs[:, C_BK + 256 * g:C_BK + 256 * g + 256]
            dve_ts(toep[:, h, :], bk, -0.5, NEG, ALU.is_lt, ALU.mult, ["cs"], [("toep", h)])
            for b in BUCKETS[g]:
                dve_ts(r_tmp.ap(), bk, float(b), r_rb.ap(b * 6 + h, b * 6 + h + 1), ALU.is_equal, ALU.mult,
                       ["cs"] + r_rb.keys(), r_tmp.keys())
                dve_tt(toep[:, h, :], toep[:, h, :], r_tmp.ap(), ALU.add, r_tmp.keys() + [("toep", h)],
                       [("toep", h)])
        for h in range(6 if CUT >= 3 else 0):
            mcol = C_MN + (0 if h < 2 else 64)
            dve_tt(bnew[:, h, :], toep[0:64, h, 128:192], cs[0:64, mcol:mcol + 64], ALU.add,
                   [("toep", h), "cs"], ["bnew"])
        sbias = bc0[:, 0, 0:24].rearrange("p (h t) -> p h t", t=4)
        for h in range(6 if CUT >= 3 else 0):
            if h < 2:
                dve_copy(sbias[:, h, :], toep[:, h, 0:4], [("toep", h)], ["bc0"])
            else:
                dve_copy(sbias[:, h, :], toep[:, h, 0:1].to_broadcast([128, 4]), [("toep", h)], ["bc0"])

        def rms_stats(src_chunk, src_keys, ncol, sc, out_rstd):
            acc = sc.get(512, F32)
            sq = [sc.get(512, F32), sc.get(512, F32)]
            for c in range(KC):
                if c == 0:
                    act(acc.ap(0, ncol), src_chunk(c), AF.Square, src_keys(c), acc.keys())
                else:
                    t = sq[c % 2]
                    act(t.ap(0, ncol), src_chunk(c), AF.Square, src_keys(c), t.keys())
                    dve_tt(acc.ap(0, ncol), acc.ap(0, ncol), t.ap(0, ncol), ALU.add, acc.keys() + t.keys(), acc.keys())
            mm(psb[7][:, 0:ncol], ones_f[:, :], acc.ap(0, ncol), True, True, acc.keys() + ["ones_f"], [PS(7)])
            act(acc.ap(0, ncol), psb[7][:, 0:ncol], AF.Sqrt, [PS(7)], acc.keys(), bias=EPS_AP(), scale=1.0 / D)
            dve_recip(out_rstd.ap(0, ncol), acc.ap(0, ncol), acc.keys(), out_rstd.keys())

        eps_t = sb("eps_t", [128, 1], F32)
        dve_memset(eps_t[:, :], EPS, ["eps"])
        EPS_AP = lambda: eps_t[:, 0:1]

        def xs_chunk(ncol):
            return (lambda c: xT[:, c, 0:ncol]), (lambda c: [("x", c)])

        R_HT = Reg(A_HT, KC * 512, BF16)
        R_MG = Reg(A_MERGED, 43 * 512, BF16)
        R_OF = Reg(A_HT, KC * 512, F32)

        def hT(c, ncol, p0=0, p1=128):
            return R_HT.ap(c * 512, c * 512 + ncol, p0, p1)

        def hTk(c):
            return R_HT.keys(c * 512, c * 512 + 512)

        def mg(c, ncol, p0=0, p1=128):
            return R_MG.ap(c * 512, c * 512 + ncol, p0, p1)

        def mgk(c):
            return R_MG.keys(c * 512, c * 512 + 512)

        def norm_to_h(l, pcol, ncol, sc_slot0):
            sc = Alloc(sc_slot0, A_END)
            rstd = sc.get(512, F32)
            ch, ck = xs_chunk(ncol)
            rms_stats(ch, ck, ncol, sc, rstd)
            for c in range(KC):
                dve_stt(hT(c, ncol), xT[:, c, 0:ncol], prm[:, l, pcol + c:pcol + c + 1], rstd.ap(0, ncol),
                        ALU.mult, ALU.mult, [("x", c), "prm"] + rstd.keys(), hTk(c))

        def dense_fm(l, woff, nk, nout, rhs_fn, rhs_keys, ncol, consume, cols_per_blk=256, kc_split=1,
                     blk_order=None, blk_src=None):
            pass

        def layer(l, ti, ncol, sample):
            pend_box = []
            layer_body(l, ti, ncol, sample, pend_box)
            for fn in pend_box:
                fn()

        def layer_body(l, ti, ncol, sample, pend_box):
            wl = wst[l]
            bank = {"n": 0}

            def nextbank():
                b = bank["n"] % 4
                bank["n"] += 1
                return b

            blk_no = {"n": 0}

            def wblock(off, nelem):
                conv_tile = blk_no["n"] % 2
                blk_no["n"] += 1
                use_scr = sample or ti > conv_tile
                first_pass = (not sample) and ti == conv_tile
                me = blk_no["n"]
                while wb_pend and wb_pend[0].blk <= me - 2:
                    wb_pend.pop(0)()
                if use_scr:
                    slot = wstate["n"] % WSLOTS
                    wstate["n"] += 1
                    dma("pool", wbuf[:, slot, 0:nelem], wscr[l][:, off:off + nelem], [("wscr", l, off)], [("w", slot)])
                    return slot
                slot = wload(wl[:, off:off + nelem], nelem)
                if first_pass:
                    def wb(off=off, nelem=nelem, slot=slot):
                        dma("sp", wscr[l][:, off:off + nelem], wbuf[:, slot, 0:nelem], [("w", slot)], [("wscr", l, off)])
                    wb.blk = me
                    wb_pend.append(wb)
                return slot

            wb_pend = pend_box

            if sample:
                for k7 in range(7):
                    n14 = min(14, 86 - 14 * k7)
                    dma("sp", xT[:, 3 + k7, 64:64 + 32 * n14],
                        scT_d[l, :, 14 * k7:14 * k7 + n14, :, :].rearrange("p c b r -> p (c b r)"), (), [("x", 3 + k7)])
            norm_to_h(l, P_PRE, ncol, A_SCR)

            sc = Alloc(A_REST, A_HT)
            qT = sc.get(6 * 512, BF16)
            ybuf = sc.get(6 * 512, F32)
            sc2 = Alloc(A_SCR, A_SCR + 6)
            s0 = ti * T

            def w_in_block(j):
                return wblock(W_IN_OFF + j * 4096, 4096)

            def fm_chunks(jblk, handler, first_chunk):
                slot = w_in_block(jblk)
                wv = wview(slot, KC, 256)
                for q in range(2):
                    b = nextbank()
                    for k in range(KC):
                        mm(psb[b][:, 0:ncol], wv[:, k, q * 128:(q + 1) * 128], hT(k, ncol), k == 0, k == KC - 1,
                           [("w", slot)] + hTk(k), [PS(b)])
                    handler(first_chunk + q, b)

            ntb = (ncol + 127) // 128
            tbw = min(ncol, 128)
            kst = [sc2.get(256, F32) for _ in range(4)]
            kbf = [sc2.get(256, BF16) for _ in range(2)]
            KT_OFF = l * 7424
            KT_G = (0, 1280, 3328)
            KT_W = (640, 1024, 2048)

            vsn = Reg(A_END - 2, 768, BF16) if sample else None

            def kt_ap(g, s, lo, hi, step=1):
                base = KT_OFF + KT_G[g] + s * KT_W[g]
                if sample:
                    base = 16000 + (2 * g + s) * 64
                if step == 1:
                    return kvreg[:, base + lo:base + hi]
                n = (hi - lo + step - 1) // step
                q_, r_ = divmod(lo, step)
                return kvreg[:, base + q_ * step:base + (q_ + n) * step].rearrange("p (n s) -> p n s", s=step)[:, :, r_]

            def kt_cur0(g):
                if sample:
                    return 0
                return (128, 512, s0)[g]

            if not sample and ti > 0:
                for s in range(2):
                    dve_copy(kt_ap(0, s, 0, 128), kt_ap(0, s, 512, 640), [("kt", l, 0)], [("kt", l, 0)])
                    act(kt_ap(1, s, 0, 512), kt_ap(1, s, 512, 1024), AF.Copy, [("kt", l, 1)], [("kt", l, 1)])
            cnt = 0
            for j in range(6):
                isv, g = j >= 3, j % 3
                slot = w_in_block(j)
                wv = wview(slot, KC, 256)
                for tb in range(ntb):
                    b = nextbank()
                    for k in range(KC):
                        mm(psb[b][0:tbw, 0:256], hT(k, ncol)[:, tb * 128:tb * 128 + tbw], wv[:, k, :],
                           k == 0, k == KC - 1, [("w", slot)] + hTk(k), [PS(b)])
                    st = kst[cnt % 4]
                    kb = kbf[cnt % 2]
                    cnt += 1
                    tok0 = s0 + tb * 128
                    if sample:
                        act(st.ap(p1=tbw), psb[b][0:tbw, 0:256], AF.Copy, [PS(b)], st.keys())
                        dma("sp", osa[g][l, :, (256 if isv else 0):(256 if isv else 0) + 256],
                            st.ap(p1=tbw), st.keys(), [], is_out=True)
                    else:
                        keep = (128, 512, 2048)[g]
                        if tok0 >= SEQ - keep:
                            act(st.ap(), psb[b][:, 0:256], AF.Copy, [PS(b)], st.keys())
                            r0 = tok0 - (SEQ - keep)
                            dma("sp", oa[g][l, r0:r0 + 128, (256 if isv else 0):(256 if isv else 0) + 256],
                                st.ap(), st.keys(), [], is_out=True)
                    if isv:
                        if sample:
                            dve_copy(vsn.ap(g * 256, (g + 1) * 256, p1=tbw), psb[b][0:tbw, 0:256], [PS(b)], vsn.keys())
                        else:
                            dve_copy(kb.ap(), psb[b][:, 0:256], [PS(b)], kb.keys())
                            dma("sp", vscr[l, tok0:tok0 + 128, g * 256:(g + 1) * 256], kb.ap(), kb.keys(),
                                [("vscr", l, ti)])
                    else:
                        dve_copy(kb.ap(p1=tbw), psb[b][0:tbw, 0:256], [PS(b)], kb.keys())
                        pt = psb[4 + (cnt % 2)]
                        ptb = pt[:].bitcast(BF16)
                        for s in range(2):
                            tr(ptb[:, s * 128:s * 128 + tbw], kb.ap(s * 128, s * 128 + 128, p1=tbw), ident[0:tbw, 0:tbw],
                               kb.keys() + ["ident"], [PS(4 + (cnt % 2))])
                        c0 = kt_cur0(g) + tb * 128
                        for s in range(2):
                            act(kt_ap(g, s, c0, c0 + tbw), ptb[:, s * 128:s * 128 + tbw], AF.Copy,
                                [PS(4 + (cnt % 2))], [("kt", l, g)] if not sample else ["ktnew"])

            def q_handler(c, b):
                act(qT.ap(c * 512, c * 512 + ncol), psb[b][:, 0:ncol], AF.Copy, [PS(b)], qT.keys(c * 512, c * 512 + 512))
            for j in range(3):
                fm_chunks(6 + j, q_handler, 2 * j)

            if sample:
                sample_attention(l, qT, ybuf, Alloc(A_SCR + 6, A_END - 2), kt_ap, vsn)
            else:
                prompt_attention(l, ti, qT, ybuf, Alloc(A_SCR + 6, A_END), kt_ap)

            gsc = Alloc(A_END - 4, A_END)
            gt = [gsc.get(512, F32), gsc.get(512, F32)]
            gstate = {"n": 0}

            def gate_blocks(jblk0, mg0):
                def gh(c, b):
                    t = gt[gstate["n"] % 2]
                    gstate["n"] += 1
                    act(t.ap(0, ncol), psb[b][:, 0:ncol], AF.Sigmoid, [PS(b)], t.keys())
                    dve_tt(mg(mg0 + c, ncol), ybuf.ap(c * 512, c * 512 + ncol), t.ap(0, ncol), ALU.mult,
                           ybuf.keys(c * 512, c * 512 + 512) + t.keys(), mgk(mg0 + c))
                for j in range(3):
                    fm_chunks(jblk0 + j, gh, 2 * j)

            gate_blocks(9, 0)
            if stage < 2:
                return
            if stage == 13:
                scM = None
            branch_B(l, ncol, sample, fm_chunks, ybuf, Alloc(A_SCR, A_END - 4), Alloc(A_REST, A_REST + 6), nextbank)
            gate_blocks(18, 6)
            if stage == 11:
                return
            branch_C(l, ti, ncol, sample, fm_chunks, ybuf, Alloc(A_SCR, A_END - 4), Alloc(A_REST, A_REST + 6), nextbank)
            gate_blocks(24, 12)
            if stage == 12:
                return
            branch_M(l, ti, ncol, sample, fm_chunks, ybuf, Alloc(A_SCR, A_END - 4), Alloc(A_REST, A_REST + 6), nextbank)
            if stage in (14, 15):
                return
            gate_blocks(30, 18)
            if stage < 3:
                return

            def add_norm(woff, nk, ksplit, rhs, rhs_keys, pcol):
                acc = Reg(75, 512, F32)
                sqs = [Reg(77, 512, F32), Reg(77, 512, F32)]
                rstd = Reg(79, 512, F32)
                nkb = (nk + ksplit - 1) // ksplit
                off = woff
                for m in range(KC):
                    b = nextbank()
                    k0 = 0
                    for part in range(ksplit):
                        kn = min(nkb, nk - k0)
                        slot = wblock(off, kn * 128)
                        off += kn * 128
                        wv = wview(slot, kn, 128)
                        for k in range(kn):
                            mm(psb[b][:, 0:ncol], wv[:, k, :], rhs(k0 + k), (k0 + k) == 0, (k0 + k) == nk - 1,
                               [("w", slot)] + rhs_keys(k0 + k), [PS(b)])
                        k0 += kn
                    act(R_OF.ap(m * 512, m * 512 + ncol), psb[b][:, 0:ncol], AF.Copy, [PS(b)],
                        R_OF.keys(m * 512, m * 512 + 512))
                    if m == 0:
                        act(acc.ap(0, ncol), psb[b][:, 0:ncol], AF.Square, [PS(b)], acc.keys())
                    else:
                        t = sqs[m % 2]
                        act(t.ap(0, ncol), psb[b][:, 0:ncol], AF.Square, [PS(b)], t.keys())
                        dve_tt(acc.ap(0, ncol), acc.ap(0, ncol), t.ap(0, ncol), ALU.add, acc.keys() + t.keys(),
                               acc.keys())
                mm(psb[7][:, 0:ncol], ones_f[:, :], acc.ap(0, ncol), True, True, acc.keys() + ["ones_f"], [PS(7)])
                act(acc.ap(0, ncol), psb[7][:, 0:ncol], AF.Sqrt, [PS(7)], acc.keys(), bias=EPS_AP(), scale=1.0 / D)
                dve_recip(rstd.ap(0, ncol), acc.ap(0, ncol), acc.keys(), rstd.keys())
                for m in range(KC):
                    t = sqs[m % 2]
                    dve_stt(t.ap(0, ncol), R_OF.ap(m * 512, m * 512 + ncol), prm[:, l, pcol + m:pcol + m + 1],
                            rstd.ap(0, ncol), ALU.mult, ALU.mult,
                            R_OF.keys(m * 512, m * 512 + 512) + ["prm"] + rstd.keys(), t.keys())
                    dve_tt(xT[:, m, 0:ncol], xT[:, m, 0:ncol], t.ap(0, ncol), ALU.add, [("x", m)] + t.keys(),
                           [("x", m)])

            add_norm(W_OUT_OFF, 24, 1, lambda k: mg(k, ncol), mgk, P_POSTM)
            if stage < 4:
                return
            norm_to_h(l, P_PREF, ncol, A_REST)
            scF = Alloc(A_SCR, A_END)
            ua = [scF.get(1024, F32), scF.get(1024, F32)]
            tt = [scF.get(512, F32) for _ in range(4)]
            nb_ = NB if sample else 1
            tpb = ncol // nb_
            W = tpb + 2
            for j in range(NPAIR):
                slot = wblock(W_UP_OFF + j * 4096, 4096)
                wv = wview(slot, KC, 256)
                res = []
                for ab in range(2):
                    ch = j + ab * NPAIR
                    b = nextbank()
                    for k in range(KC):
                        mm(psb[b][:, 0:ncol], wv[:, k, ab * 128:(ab + 1) * 128], hT(k, ncol), k == 0, k == KC - 1,
                           [("w", slot)] + hTk(k), [PS(b)])
                    u = ua[ab]
                    uv = u.ap(0, nb_ * W).rearrange("p (b w) -> p b w", w=W)
                    pv = psb[b][:, 0:ncol].rearrange("p (b t) -> p b t", t=tpb)
                    cw = lambda t_: prm[:, l, P_CW + t_ * 86 + ch:P_CW + t_ * 86 + ch + 1]
                    cbias = prm[:, l, P_CB + ch:P_CB + ch + 1]
                    t1 = tt[2 * ab]
                    t1v = t1.ap(0, ncol).rearrange("p (b t) -> p b t", t=tpb)
                    if sample:
                        gc_, go_ = 3 + ch // 14, 64 + (ch % 14) * 32
                        gap = xT[:, gc_, go_:go_ + 32].rearrange("p (b r) -> p b r", r=2)
                        act(uv[:, :, 0:2], gap, AF.Copy, [("x", gc_)], u.keys())
                    else:
                        act(uv[:, :, 0:2], cpref[:, l, ch:ch + 1, :], AF.Copy, [("cpref", l, ch)], u.keys())
                    act(uv[:, :, 2:W], pv, AF.Copy, [PS(b)], u.keys())
                    act(t1v, pv, AF.Identity, [PS(b), "prm"], t1.keys(), bias=cbias, scale=cw(2))
                    dve_stt(t1v, uv[:, :, 1:W - 1], cw(1), t1v, ALU.mult, ALU.add, u.keys() + t1.keys() + ["prm"],
                            t1.keys())
                    dve_stt(t1v, uv[:, :, 0:W - 2], cw(0), t1v, ALU.mult, ALU.add, u.keys() + t1.keys() + ["prm"],
                            t1.keys())
                    if sample:
                        act(gap, uv[:, :, W - 2:W], AF.Copy, u.keys(), [("x", gc_)])
                    else:
                        act(cpref[:, l, ch:ch + 1, :], uv[:, :, W - 2:W], AF.Copy, u.keys(), [("cpref", l, ch)])
                    res.append(t1)
                ga = tt[1]
                act(ga.ap(0, ncol), res[0].ap(0, ncol), AF.Gelu_apprx_tanh, res[0].keys(), ga.keys())
                dve_tt(mg(j, ncol), ga.ap(0, ncol), res[1].ap(0, ncol), ALU.mult, ga.keys() + res[1].keys(), mgk(j))
            if not sample and ti == NT - 1:
                dma("sp", oconv[l], cpref[:, l, :, :], [("cpref", l, c) for c in range(86)], [], is_out=True)
            if sample:
                for k7 in range(7):
                    n14 = min(14, 86 - 14 * k7)
                    dma("sp", osconv[l, :, 14 * k7:14 * k7 + n14, :, :].rearrange("p c b r -> p (c b r)"),
                        xT[:, 3 + k7, 64:64 + 32 * n14], [("x", 3 + k7)], [], is_out=True)
            if stage < 5:
                return
            add_norm(W_DN_OFF, NPAIR, 2, lambda k: mg(k, ncol), mgk, P_POSTF)

        def prompt_attention(l, ti, qT, ybuf, sc2, kt_ap):
            s0 = ti * T
            vt = sc2.get(16 * 256, BF16)
            stmp = [sc2.get(256, F32), sc2.get(256, F32)]
            ptb = [sc2.get(256, BF16), sc2.get(256, BF16)]
            rden = sc2.get(1024, F32)
            DEN = (4, 5)
            vl = vscr[l]
            ucnt = {"n": 0}
            den_started = [False, False]

            def vrows(tile_idx, r0, step, nk, g):
                src = bass.AP(vl.tensor, vl.offset + r0 * 768 + g * 256, [[step * 768, nk], [1, 256]])
                t0, t1 = r0 // T, (r0 + step * (nk - 1)) // T
                dma("sp", vt.ap(tile_idx * 256, tile_idx * 256 + 256, p1=nk), src,
                    [("vscr", l, t) for t in range(t0, t1 + 1)], vt.keys(tile_idx * 256, tile_idx * 256 + 256))

            def unit(g, s, kparts, qap, nq, num_out, den_out, first_num):
                h = 2 * g + s
                i = ucnt["n"] % 2
                ucnt["n"] += 1
                sb_ = 6 + i
                col = 0
                spans = []
                for (kap, nk, bias, vi) in kparts:
                    mm(psb[sb_][0:nk, col:col + nq], kap, qap, True, True, [("kt", l, g)] + qT.keys(), [PS(sb_)])
                    dve_stt(stmp[i].ap(col, col + nq, p1=nk), psb[sb_][0:nk, col:col + nq], 128.0 ** -0.5, bias,
                            ALU.mult, ALU.add, [PS(sb_), ("toep", h)], stmp[i].keys())
                    act(ptb[i].ap(col, col + nq, p1=nk), stmp[i].ap(col, col + nq, p1=nk), AF.Exp, stmp[i].keys(),
                        ptb[i].keys())
                    spans.append((col, nk, vi))
                    col += nq
                def pv_phase():
                    for n_, (c0, nk, vi) in enumerate(spans):
                        mm(num_out, vt.ap(vi * 256 + s * 128, vi * 256 + s * 128 + 128, p1=nk), ptb[i].ap(c0, c0 + nq, p1=nk),
                           first_num and n_ == 0, True, vt.keys(vi * 256, vi * 256 + 256) + ptb[i].keys(), [PS(s)])
                        mm(den_out, ones_b[0:nk, :], ptb[i].ap(c0, c0 + nq, p1=nk), not den_started[s], True,
                           ["ones_b"] + ptb[i].keys(), [PS(DEN[s])])
                        den_started[s] = True
                while pend:
                    pend.pop(0)()
                pend.append(pv_phase)

            pend = []

            def flush():
                while pend:
                    pend.pop(0)()

            def colsel(ap2d, start, step, n):
                if step == 1:
                    return ap2d[:, start:start + n]
                q_, r_ = divmod(start, step)
                return ap2d[:, q_ * step:(q_ + n) * step].rearrange("p (n s) -> p n s", s=step)[:, :, r_]

            for g in range(3):
                d = A_DIL[g]
                if g == 0:
                    for k in range(5):
                        r0 = s0 - 128 + 128 * k
                        if r0 >= 0:
                            vrows(k, r0, 1, 128, 0)
                elif g == 1:
                    for r in range(4):
                        if ti > 0:
                            vrows(5 + 2 * r, s0 - 512 + r, 4, 128, 1)
                        vrows(5 + 2 * r + 1, s0 + r, 4, 128, 1)
                else:
                    for r in range(16):
                        vrows(r, r, 16, 32 * (ti + 1), 2)
                for s in range(2):
                    h = 2 * g + s
                    numb = psb[s]
                    denb = psb[DEN[s]]
                    qh = qT.ap(h * 512, h * 512 + 512)
                    first = True
                    if g == 0:
                        for qb in range(4):
                            parts = []
                            if ti > 0 or qb > 0:
                                parts.append((kt_ap(0, s, 128 * qb, 128 * qb + 128), 128, toep[:, h, 0:128], qb))
                            parts.append((kt_ap(0, s, 128 * qb + 128, 128 * qb + 256), 128, toep[:, h, 128:256], qb + 1))
                            unit(g, s, parts, qh[:, qb * 128:qb * 128 + 128], 128, numb[:, qb * 128:qb * 128 + 128],
                                 denb[:, qb * 128:qb * 128 + 128], first)
                            first = False
                    elif g == 1:
                        for r in range(4):
                            parts = []
                            if ti > 0:
                                parts.append((kt_ap(1, s, r, 512, 4), 128, toep[:, h, 0:128], 5 + 2 * r))
                            parts.append((kt_ap(1, s, 512 + r, 1024, 4), 128, toep[:, h, 128:256], 5 + 2 * r + 1))
                            unit(g, s, parts, colsel(qh, r, 4, 128), 128, colsel(numb[:, :], r, 4, 128),
                                 colsel(denb[:, :], r, 4, 128), first)
                            first = False
                    else:
                        nk = 32 * (ti + 1)
                        for r in range(16):
                            parts = [(kt_ap(2, s, r, r + 16 * nk, 16), nk, toep[0:nk, h, 128 + 32 * ti:128 + 32 * ti + 32], r)]
                            unit(g, s, parts, colsel(qh, r, 16, 32), 32, colsel(numb[:, :], r, 16, 32),
                                 colsel(denb[:, :], r, 16, 32), first)
                            first = False
                    flush()
                    act(ybuf.ap(h * 512, h * 512 + 512), numb[:, :], AF.Copy, [PS(s)], ybuf.keys(h * 512, h * 512 + 512))
            for s in range(2):
                dve_recip(rden.ap(s * 512, s * 512 + 512), psb[DEN[s]][:, :], [PS(DEN[s])], rden.keys(s * 512, s * 512 + 512))
            for h in range(6):
                s = h % 2
                dve_tt(ybuf.ap(h * 512, h * 512 + 512), ybuf.ap(h * 512, h * 512 + 512), rden.ap(s * 512, s * 512 + 512),
                       ALU.mult, ybuf.keys(h * 512, h * 512 + 512) + rden.keys(s * 512, s * 512 + 512),
                       ybuf.keys(h * 512, h * 512 + 512))

        def sample_attention(l, qT, ybuf, sc, kt_ap, vsn):
            tmpn = sc.get(384, F32)
            pnew = sc.get(384, BF16)
            tmpc = [sc.get(32, F32), sc.get(32, F32)]
            pc = [sc.get(32, BF16), sc.get(32, BF16)]
            rden = sc.get(128, F32)
            NUM, DEN = 0, 1
            SCL = 128.0 ** -0.5
            qh = lambda h: qT.ap(h * 512, h * 512 + 64)
            for h in range(6):
                g, s = h // 2, h % 2
                mm(psb[6][0:64, h * 64:(h + 1) * 64], kt_ap(g, s, 0, 64), qh(h), True, True, ["ktnew"] + qT.keys(), [PS(6)])
            dve_stt(tmpn.ap(p1=64), psb[6][0:64, 0:384], SCL, bnew[:, :, :].rearrange("p h c -> p (h c)"), ALU.mult, ALU.add,
                    [PS(6), "bnew"], tmpn.keys())
            act(pnew.ap(p1=64), tmpn.ap(p1=64), AF.Exp, tmpn.keys(), pnew.keys())
            for h in range(6):
                mm(psb[NUM][:, h * 64:(h + 1) * 64], vsn.ap(h * 128, (h + 1) * 128, p1=64), pnew.ap(h * 64, (h + 1) * 64, p1=64),
                   h == 0, True, vsn.keys() + pnew.keys(), [PS(NUM)])
            for h in range(6):
                s = h % 2
                mm(psb[DEN][:, s * 64:(s + 1) * 64], ones_b[0:64, :], pnew.ap(h * 64, (h + 1) * 64, p1=64), h == 0, True,
                   ["ones_b"] + pnew.keys(), [PS(DEN)])
            def ck(i, si, s):
                o = i * 4608 + (si * 2 + s) * 128
                return kvreg[:, o:o + 128]

            def cv(i, si, s):
                o = i * 4608 + 2304 + (si * 2 + s) * 128
                return kvreg[:, o:o + 128]
            for b in range(NB):
                i = b % 2
                alias = [("kt", l_, g_) for l_ in range(2) for g_ in range(3)] if b < 2 else []
                dma("pool", kvreg[:, i * 4608:i * 4608 + 2304], caK_d[l, b], (), [("cak", i)] + alias)
                dma("pool", kvreg[:, i * 4608 + 2304:i * 4608 + 4608], caV_d[l, b], (), [("cav", i)] + alias)
                pb = 6 + b % 2
                for h in range(6):
                    g, s = h // 2, h % 2
                    if g == 0:
                        mm(psb[pb][:, h * 4:h * 4 + 4], ck(i, 0, s), qh(h)[:, 4 * b:4 * b + 4], True, True,
                           [("cak", i)] + qT.keys(), [PS(pb)])
                    else:
                        for t in range(4):
                            mm(psb[pb][:, h * 4 + t:h * 4 + t + 1], ck(i, 1 + 4 * (g - 1) + t, s), qh(h)[:, 4 * b + t:4 * b + t + 1],
                               True, True, [("cak", i)] + qT.keys(), [PS(pb)])
                dve_stt(tmpc[i].ap(0, 24), psb[pb][:, 0:24], SCL, bc0[:, 0, 0:24], ALU.mult, ALU.add, [PS(pb), "bc0"], tmpc[i].keys())
                act(pc[i].ap(0, 24), tmpc[i].ap(0, 24), AF.Exp, tmpc[i].keys(), pc[i].keys())
                for h in range(6):
                    g, s = h // 2, h % 2
                    c0 = h * 64 + 4 * b
                    if g == 0:
                        mm(psb[NUM][:, c0:c0 + 4], cv(i, 0, s), pc[i].ap(h * 4, h * 4 + 4), False, True,
                           [("cav", i)] + pc[i].keys(), [PS(NUM)])
                    else:
                        for t in range(4):
                            mm(psb[NUM][:, c0 + t:c0 + t + 1], cv(i, 1 + 4 * (g - 1) + t, s), pc[i].ap(h * 4 + t, h * 4 + t + 1),
                               False, True, [("cav", i)] + pc[i].keys(), [PS(NUM)])
                    mm(psb[DEN][:, s * 64 + 4 * b:s * 64 + 4 * b + 4], ones_b[:, :], pc[i].ap(h * 4, h * 4 + 4), False, True,
                       ["ones_b"] + pc[i].keys(), [PS(DEN)])
            dve_recip(rden.ap(), psb[DEN][:, 0:128], [PS(DEN)], rden.keys())
            for h in range(6):
                s = h % 2
                dve_tt(ybuf.ap(h * 512, h * 512 + 64), psb[NUM][:, h * 64:(h + 1) * 64], rden.ap(s * 64, s * 64 + 64), ALU.mult,
                       [PS(NUM)] + rden.keys(), ybuf.keys(h * 512, h * 512 + 512))

        def seg_evac_mul(dst_fn, ps_ap_fn, other_fn, segs, ncol, reads, writes):
            pass

        def branch_B(l, ncol, sample, fm_chunks, ybuf, sc, scy, nextbank):
            vn_tok = scy.get(6 * 512, BF16)
            vnT = sc.get(6 * 512, BF16)
            mean = sc.get(512, F32)
            rstd = sc.get(512, F32)
            tmp = [sc.get(512, F32), sc.get(512, F32)]
            S1, S2 = 4, 5
            ntb = (ncol + 127) // 128
            tbw = min(ncol, 128)
            yk = lambda c: ybuf.keys(c * 512, c * 512 + 512)
            ya = lambda c, p0=0, p1=128: ybuf.ap(c * 512, c * 512 + ncol, p0, p1)

            def vh(c, b):
                act(ya(c), psb[b][:, 0:ncol], AF.Gelu_apprx_tanh, [PS(b)], yk(c))
                mm(psb[S1][:, 0:ncol], ones_f[:, :], ya(c), c == 0, c == 5, yk(c) + ["ones_f"], [PS(S1)])
                t = tmp[c % 2]
                act(t.ap(0, ncol), ya(c), AF.Square, yk(c), t.keys())
                mm(psb[S2][:, 0:ncol], ones_f[:, :], t.ap(0, ncol), c == 0, c == 5, t.keys() + ["ones_f"], [PS(S2)])
            for j in range(3):
                fm_chunks(15 + j, vh, 2 * j)
            m_, r_, t0, t1 = mean.ap(0, ncol), rstd.ap(0, ncol), tmp[0].ap(0, ncol), tmp[1].ap(0, ncol)
            dve_ts(m_, psb[S1][:, 0:ncol], 1.0 / 768, None, ALU.mult, ALU.bypass, [PS(S1)], mean.keys())
            dve_tt(t0, m_, m_, ALU.mult, mean.keys(), tmp[0].keys())
            dve_stt(t1, psb[S2][:, 0:ncol], 1.0 / 768, t0, ALU.mult, ALU.subtract, [PS(S2)] + tmp[0].keys(), tmp[1].keys())
            act(t1, t1, AF.Sqrt, tmp[1].keys(), tmp[1].keys(), bias=EPS_AP(), scale=1.0)
            dve_recip(r_, t1, tmp[1].keys(), rstd.keys())
            for c in range(6):
                t = tmp[c % 2]
                dve_tt(t.ap(0, ncol), ya(c), m_, ALU.subtract, yk(c) + mean.keys(), t.keys())
                if sample:
                    dve_stt(ya(c), t.ap(0, ncol), prm[:, l, P_GV + c:P_GV + c + 1], r_, ALU.mult, ALU.mult,
                            t.keys() + rstd.keys() + ["prm"], yk(c))
                    dma("sp", osv[l, :, c, :], ya(c), yk(c), [], is_out=True)
                    dve_copy(vnT.ap(c * 512, c * 512 + ncol), ya(c), yk(c), vnT.keys(c * 512, c * 512 + 512))
                else:
                    dve_stt(vnT.ap(c * 512, c * 512 + ncol), t.ap(0, ncol), prm[:, l, P_GV + c:P_GV + c + 1], r_,
                            ALU.mult, ALU.mult, t.keys() + rstd.keys() + ["prm"], vnT.keys(c * 512, c * 512 + 512))
            for tb in range(ntb):
                pb = 6 + tb % 2
                ptv = psb[pb][:].bitcast(BF16)
                for c in range(6):
                    tr(ptv[0:tbw, c * 128:(c + 1) * 128], vnT.ap(c * 512 + tb * 128, c * 512 + tb * 128 + tbw), ident[:, :],
                       vnT.keys(c * 512, c * 512 + 512) + ["ident"], [PS(pb)])
                act(vn_tok.ap(tb * 768, tb * 768 + 768, p1=tbw), ptv[0:tbw, 0:768], AF.Copy, [PS(pb)],
                    vn_tok.keys(tb * 768, tb * 768 + 768))
            for c in range(6):
                b = nextbank()
                for (p0, p1, g, lo, hi) in SEGS[c]:
                    first = True
                    for tb in range(ntb):
                        if sample:
                            rhs_w, rhs_b = wbig[0:tbw, l, g, :], bsrow[0:1, l, g, :]
                            rk = ["wbig"]
                        else:
                            rhs_w, rhs_b = wsT[:, l, g, :], bsb[0:1, l, g, :]
                            rk = ["wsT", "bsb"]
                        f0 = g * 192 + lo
                        mm(psb[b][p0:p1, tb * 128:tb * 128 + tbw], vn_tok.ap(tb * 768 + f0, tb * 768 + f0 + (hi - lo), p1=tbw),
                           rhs_w, first, False, vn_tok.keys(tb * 768, tb * 768 + 768) + rk, [PS(b)])
                        mm(psb[b][p0:p1, tb * 128:tb * 128 + tbw], ones_b[0:1, 0:p1 - p0], rhs_b, False, True,
                           ["ones_b"] + rk, [PS(b)])
                        first = False
                act(ya(c), psb[b][:, 0:ncol], AF.Copy, [PS(b)], yk(c))
            def uh(c, b):
                t = tmp[c % 2]
                act(t.ap(0, ncol), psb[b][:, 0:ncol], AF.Gelu_apprx_tanh, [PS(b)], t.keys())
                dve_tt(ya(c), ya(c), t.ap(0, ncol), ALU.mult, yk(c) + t.keys(), yk(c))
            for j in range(3):
                fm_chunks(12 + j, uh, 2 * j)

        def branch_C(l, ti, ncol, sample, fm_chunks, ybuf, sc, scy, nextbank):
            dT = scy.get(6 * 512, BF16)
            nb_ = NB if sample else 1
            tpb = ncol // nb_
            W = tpb + 15
            ext = [sc.get(1024, F32), sc.get(1024, F32)]
            sa_ = sc.get(1024, F32)
            sb_ = sc.get(1024, F32)
            LEV = {0: 1, 1: 2, 2: 2, 3: 3, 4: 4, 5: 4}
            LOWLEV = {1: 1, 4: 3}
            WIN = (2, 4, 8, 16)

            def v3(reg, lo=0, hi=None, p0=0, p1=128):
                hi = W if hi is None else hi
                return reg.ap(0, nb_ * W, p0, p1).rearrange("p (b w) -> p b w", w=W)[:, :, lo:hi]

            def ch(c, b):
                e = ext[c % 2]
                pv = psb[b][:, 0:ncol].rearrange("p (b t) -> p b t", t=tpb)
                if sample:
                    pgap = xT[:, 10 + c, 64:304].rearrange("p (b r) -> p b r", r=15)
                    dma("sp", xT[:, 10 + c, 64:304], spT_d[l, :, c, :, :].rearrange("p b r -> p (b r)"), (), [("x", 10 + c)])
                    dve_copy(v3(e, 0, 15), pgap, [("x", 10 + c)], e.keys())
                else:
                    dve_copy(v3(e, 0, 15), ppref[:, l, c:c + 1, :], [("ppref", l, c)], e.keys())
                act(v3(e, 15, W), pv, AF.Copy, [PS(b)], e.keys())
                if sample:
                    dve_copy(pgap, v3(e, tpb, W), e.keys(), [("x", 10 + c)])
                    dma("sp", ospool[l, :, c, :, :].rearrange("p b r -> p (b r)"), xT[:, 10 + c, 64:304], [("x", 10 + c)], [], is_out=True)
                else:
                    dve_copy(ppref[:, l, c:c + 1, :], v3(e, W - 15, W), e.keys(), [("ppref", l, c)])
                    if ti == NT - 1:
                        dma("sp", opool[l, :, c, :], e.ap(W - 15, W), e.keys(), [], is_out=True)
                cur, nxt, sh = e, sa_, 1
                for lev in range(1, LEV[c] + 1):
                    p0 = 64 if (c in LOWLEV and lev > LOWLEV[c]) else 0
                    dve_tt(v3(nxt, sh * 2 - 1, W, p0), v3(cur, sh * 2 - 1, W, p0), v3(cur, sh - 1, W - sh, p0), ALU.add,
                           cur.keys() + nxt.keys(), nxt.keys())
                    if c in LOWLEV and lev == LOWLEV[c]:
                        low = nxt
                    cur, nxt = nxt, (sb_ if nxt is sa_ else sa_)
                    sh *= 2
                for (p0, p1, g, lo, hi) in SEGS[c]:
                    src = low if (c in LOWLEV and p0 == 0) else cur
                    w = WIN[g]
                    if not sample and ti == 0:
                        fc = cs[p0:p1, C_FAC + 16 * g:C_FAC + 16 * g + 16]
                        dve_tt(src.ap(15, 31, p0, p1), src.ap(15, 31, p0, p1), fc, ALU.mult, src.keys() + ["cs"], src.keys())
                    dve_stt(dT.ap(c * 512, c * 512 + ncol, p0, p1).rearrange("p (b t) -> p b t", t=tpb),
                            v3(src, 15, W, p0, p1), 1.0 / w, v3(e, 15, W, p0, p1), ALU.mult, ALU.subtract,
                            src.keys() + e.keys(), dT.keys(c * 512, c * 512 + 512))
            for j in range(3):
                fm_chunks(21 + j, ch, 2 * j)
            for m in range(6):
                b = nextbank()
                for (p0, p1, g, lo, hi) in SEGS[m]:
                    ks = sorted(GSEGS[g], key=lambda t_: -(t_[2] - t_[1]))
                    for n_, (cc, kp0, kp1) in enumerate(ks):
                        mm(psb[b][p0:p1, 0:ncol], wcb[kp0:kp1, l, cc, lo:hi], dT.ap(cc * 512, cc * 512 + ncol, kp0, kp1),
                           n_ == 0, n_ == 1, ["wcb"] + dT.keys(cc * 512, cc * 512 + 512), [PS(b)])
                act(ybuf.ap(m * 512, m * 512 + ncol), psb[b][:, 0:ncol], AF.Identity, [PS(b), "prm"],
                    ybuf.keys(m * 512, m * 512 + 512), scale=prm[:, l, P_CS + m:P_CS + m + 1])

        def branch_M(l, ti, ncol, sample, fm_chunks, ybuf, sc, scy, nextbank):
            qm = scy.get(6 * 512, BF16)
            pt = [sc.get(512, BF16), sc.get(512, BF16)]
            rden = sc.get(512, F32)
            def qh(c, b):
                act(qm.ap(c * 512, c * 512 + ncol), psb[b][:, 0:ncol], AF.Copy, [PS(b)], qm.keys(c * 512, c * 512 + 512))
            for j in range(3):
                fm_chunks(27 + j, qh, 2 * j)
            if sample:
                sample_memattn(l, qm, ybuf, sc)
                return
            dma("sp", memkv[:, :], mkscr[l], [("mkscr", l)], ["memkv"])
            SC = 192.0 ** -0.5
            for h in range(4):
                for mt in range(2):
                    for n_, (cc, p0, p1) in enumerate(GSEGS[h]):
                        mm(psb[4 + mt][:, 0:ncol], memkv[p0:p1, cc * 256 + mt * 128:cc * 256 + mt * 128 + 128],
                           qm.ap(cc * 512, cc * 512 + ncol, p0, p1), n_ == 0, n_ == 1,
                           ["memkv"] + qm.keys(cc * 512, cc * 512 + 512), [PS(4 + mt)])
                    act(pt[mt].ap(0, ncol), psb[4 + mt][:, 0:ncol], AF.Exp, [PS(4 + mt)], pt[mt].keys(), scale=SC)
                for mt in range(2):
                    mm(psb[6][:, 0:ncol], ones_b[:, :], pt[mt].ap(0, ncol), mt == 0, mt == 1, ["ones_b"] + pt[mt].keys(), [PS(6)])
                dve_recip(rden.ap(0, ncol), psb[6][:, 0:ncol], [PS(6)], rden.keys())
                for (cc, p0, p1) in GSEGS[h]:
                    b = nextbank()
                    f0 = 1536 + cc * 128 + p0
                    for mt in range(2):
                        mm(psb[b][p0:p1, 0:ncol], memkv[:, f0 + mt * 768:f0 + mt * 768 + (p1 - p0)], pt[mt].ap(0, ncol),
                           mt == 0, mt == 1, ["memkv"] + pt[mt].keys(), [PS(b)])
                    dve_tt(ybuf.ap(cc * 512, cc * 512 + ncol, p0, p1), psb[b][p0:p1, 0:ncol], rden.ap(0, ncol, p0, p1), ALU.mult,
                           [PS(b)] + rden.keys(), ybuf.keys(cc * 512, cc * 512 + 512))

        def sample_memattn(l, qm, ybuf, sc):
            pm = [sc.get(32, BF16), sc.get(32, BF16)]
            rden = sc.get(256, F32)
            NUM, DEN = 2, 3
            SC = 192.0 ** -0.5
            MB = 9216
            first = True
            for b in range(NB):
                i = b % 2
                ko, vo = MB + i * 3072, MB + i * 3072 + 1536
                alias = [("kt", l_, g_) for l_ in range(2) for g_ in range(3)] if b < 2 else []
                dma("pool", kvreg[:, ko:ko + 1536], cmK_d[l, b], (), [("cmk", i)] + alias)
                dma("pool", kvreg[:, vo:vo + 1536], cmV_d[l, b], (), [("cmv", i)] + alias)
                for mt in range(2):
                    for h in range(4):
                        c0 = (mt * 4 + h) * 4
                        pb = 4 + 2 * (b % 2) + (h % 2)
                        for n_, (cc, p0, p1) in enumerate(GSEGS[h]):
                            mm(psb[pb][:, c0:c0 + 4], kvreg[p0:p1, ko + cc * 256 + mt * 128:ko + cc * 256 + mt * 128 + 128],
                               qm.ap(cc * 512 + 4 * b, cc * 512 + 4 * b + 4, p0, p1), n_ == 0, n_ == 1,
                               [("cmk", i)] + qm.keys(cc * 512, cc * 512 + 512), [PS(pb)])
                for hp in range(2):
                    pb = 4 + 2 * (b % 2) + hp
                    pv_ = pm[i].ap(0, 32).rearrange("p (m h q) -> p m h q", m=2, h=4)
                    sv_ = psb[pb][:, 0:32].rearrange("p (m h q) -> p m h q", m=2, h=4)
                    for hh in (hp, hp + 2):
                        act(pv_[:, :, hh, :], sv_[:, :, hh, :], AF.Exp, [PS(pb)], pm[i].keys(), scale=SC)
                if stage == 14:
                    continue
                for h in range(4):
                    for (cc, p0, p1) in GSEGS[h]:
                        for mt in range(2):
                            c0 = (mt * 4 + h) * 4
                            f0 = vo + mt * 768 + cc * 128 + p0
                            mm(psb[NUM][p0:p1, cc * 64 + 4 * b:cc * 64 + 4 * b + 4], kvreg[:, f0:f0 + (p1 - p0)],
                               pm[i].ap(c0, c0 + 4), first, True, [("cmv", i)] + pm[i].keys(), [PS(NUM)])
                            first = False
                    if stage == 15:
                        continue
                    for mt in range(2):
                        c0 = (mt * 4 + h) * 4
                        mm(psb[DEN][:, h * 64 + 4 * b:h * 64 + 4 * b + 4], ones_b[:, :], pm[i].ap(c0, c0 + 4),
                           b == 0 and h == 0 and mt == 0, True, ["ones_b"] + pm[i].keys(), [PS(DEN)])
            if stage in (14, 15):
                return
            dve_recip(rden.ap(), psb[DEN][:, 0:256], [PS(DEN)], rden.keys())
            for h in range(4):
                for (cc, p0, p1) in GSEGS[h]:
                    dve_tt(ybuf.ap(cc * 512, cc * 512 + 64, p0, p1), psb[NUM][p0:p1, cc * 64:cc * 64 + 64],
                           rden.ap(h * 64, h * 64 + 64, p0, p1), ALU.mult, [PS(NUM)] + rden.keys(), ybuf.keys(cc * 512, cc * 512 + 512))

        def prompt_memkv():
            for c in range(KC):
                dma("sp", xT[:, c, 0:256], mpT[:, c, :], (), [("x", c)])
            for l in range(2):
                norm_to_h(l, P_MEM, 256, A_SCR)
                if CUT < 5:
                    continue
                sc = Alloc(A_SCR + 8, A_END)
                st = [sc.get(256, F32), sc.get(256, F32)]
                kb = [sc.get(256, BF16), sc.get(256, BF16)]
                n = 0
                for j in range(6):
                    slot = wload(wmk[l][:, j * 4096:(j + 1) * 4096], 4096)
                    wv = wview(slot, KC, 256)
                    for mt in range(2):
                        b = n % 4
                        for k in range(KC):
                            mm(psb[b][:, 0:256], hT(k, 256)[:, mt * 128:(mt + 1) * 128], wv[:, k, :], k == 0, k == KC - 1,
                               [("w", slot)] + hTk(k), [PS(b)])
                        s_, k_ = st[n % 2], kb[n % 2]
                        act(s_.ap(), psb[b][:, 0:256], AF.Copy, [PS(b)], s_.keys())
                        dma("sp", omkv[l, mt * 128:(mt + 1) * 128, j * 256:(j + 1) * 256], s_.ap(), s_.keys(), [], is_out=True)
                        if CUT < 6:
                            n += 1
                            continue
                        if CUT == 76:
                            if j < 3:
                                o = (n % 2) * 256
                                dve_copy(memkv[:, o:o + 256], s_.ap(), s_.keys(), [("memkv", n % 2)])
                            n += 1
                            continue
                        if CUT == 74:
                            if j < 3:
                                o = (n % 2) * 256
                                dve_copy(memkv[:, o:o + 256], psb[b][:, 0:256], [PS(b)], [("memkv", n % 2)])
                            n += 1
                            continue
                        if CUT == 75:
                            if j >= 3:
                                k_ = kb[n % 2]
                                dve_copy(k_.ap(), psb[b][:, 0:256], [PS(b)], k_.keys())
                            n += 1
                            continue
                        if CUT in (72, 73):
                            if CUT == 72 and j >= 3:
                                o = 1536 + mt * 768 + (j - 3) * 256
                                dve_copy(memkv[:, o:o + 256], psb[b][:, 0:256], [PS(b)], ["memkv"])
                            n += 1
                            continue
                        if j < 3:
                            if globals().get("KCOPY_ACT", False):
                                act(k_.ap(), psb[b][:, 0:256], AF.Copy, [PS(b)], k_.keys())
                            else:
                                dve_copy(k_.ap(), psb[b][:, 0:256], [PS(b)], k_.keys())
                            if CUT in (61, 71):
                                n += 1
                                continue
                            pb = 4 + n % 2
                            ptb_ = psb[pb][:].bitcast(BF16)
                            for q in range(2):
                                tr(ptb_[:, q * 128:(q + 1) * 128], k_.ap(q * 128, q * 128 + 128), ident[:, :],
                                   k_.keys() + ["ident"], [PS(pb)])
                                cc = 2 * j + q
                                if CUT == 62:
                                    continue
                                act(memkv[:, cc * 256 + mt * 128:cc * 256 + mt * 128 + 128], ptb_[:, q * 128:(q + 1) * 128],
                                    AF.Copy, [PS(pb)], ["memkv"])
                        elif CUT not in (63, 71):
                            o = 1536 + mt * 768 + (j - 3) * 256
                            dve_copy(memkv[:, o:o + 256], psb[b][:, 0:256], [PS(b)], ["memkv"])
                        n += 1
                if CUT >= 7 and CUT not in (71, 72):
                    dma("sp", mkscr[l], memkv[:, :], ["memkv"], [("mkscr", l)])


        if CUT >= 4:
            prompt_memkv()
        for ti in range(globals().get("NT_RUN", NT)):
            s0 = ti * T
            for c in range(KC):
                dma("sp", xT[:, c, :], xpT[:, c, s0:s0 + T], (), [("x", c)])
            for l in range(2):
                layer(l, ti, T, False)
            for c in range(KC):
                dma("sp", ypT[:, c, s0:s0 + T], xT[:, c, :], [("x", c)], [], is_out=True)

        if globals().get("RUN_SAMPLE", True):
            sa2 = Alloc(A_SCR, A_END)
            r_wb = sa2.get(2 * 4 * 64, F32)
            r_br = sa2.get(2 * 4 * 64, F32)
            dma("sp", r_wb.ap(0, 512, 0, 64), wrep_d.rearrange("p l g c -> p (l g c)"), (), r_wb.keys())
            dma("sp", r_br.ap(0, 512, 0, 1), brep_d.rearrange("p l g c -> p (l g c)"), (), r_br.keys())
            for l in range(2):
                for g in range(4):
                    dve_tt(wbig[:, l, g, :], r_wb.ap(l * 256 + g * 64, l * 256 + g * 64 + 64, 0, 64), cs[0:64, C_MB:C_MB + 64], ALU.mult,
                           r_wb.keys() + ["cs"], ["wbig"])
            dve_copy(bsrow[:, :, :, :].rearrange("p l g c -> p (l g c)"), r_br.ap(0, 512, 0, 1), r_br.keys(), ["wbig"])
            dma("sp", prm[:, :, :], par2[:, :, :], (), ["prm"])
            for c in range(KC):
                dma("sp", xT[:, c, 0:TS], xsT[:, c, :], (), [("x", c)])
            for l in range(2):
                layer(l, 0, TS, True)
            for c in range(KC):
                dma("sp", ysT[:, c, :], xT[:, c, 0:TS], [("x", c)], [], is_out=True)

        S.finish(nc, sem, dsem)
        with nc.Block() as block:
            @block.sync
            def _(e):
                S.emit("sp", e)

            @block.gpsimd
            def _(e):
                S.emit("pool", e)

            @block.tensor
            def _(e):
                S.emit("pe", e)

            @block.scalar
            def _(e):
                S.emit("act", e)

            @block.vector
            def _(e):
                S.emit("dve", e)
    return nc


def _rel_bucket(dist):
    exact = 16
    d = np.maximum(dist.astype(np.float32), 1.0)
    log_b = exact + (np.log(d / exact) / np.float32(np.log(2048 / exact)) * (32 - exact)).astype(np.int32)
    return np.where(dist < exact, dist, np.minimum(log_b, 31))


def _bucket_tables():
    out = []
    for d in A_DIL:
        j = np.arange(129)
        out.append(sorted(set(int(b) for b in _rel_bucket(j * d))))
    return out


W_IN_BLOCK_COLS = ([768 + 256 * g for g in range(3)] + [1536 + 256 * g for g in range(3)] +
                   [0, 256, 512] + [COL_G + 256 * j for j in range(3)] +
                   [COL_B + 256 * j for j in range(6)] + [COL_G + 768 + 256 * j for j in range(3)] +
                   [COL_C + 256 * j for j in range(3)] + [COL_G + 1536 + 256 * j for j in range(3)] +
                   [COL_M + 256 * j for j in range(3)] + [COL_G + 2304 + 256 * j for j in range(3)])


def _blk(w, nk):
    n = w.shape[1]
    return np.ascontiguousarray(w.reshape(nk, 128, n).transpose(1, 0, 2)).reshape(128, nk * n)


def _pack_layer(w_in, w_out, w_up, w_down):
    parts = [_blk(w_in[:, c0:c0 + 256], 16) for c0 in W_IN_BLOCK_COLS]
    parts += [_blk(w_out[:, m * 128:(m + 1) * 128], 24) for m in range(16)]
    for j in range(NPAIR):
        parts.append(_blk(np.concatenate([w_up[:, j * 128:(j + 1) * 128],
                                          w_up[:, DFF + j * 128:DFF + (j + 1) * 128]], axis=1), 16))
    for m in range(16):
        parts.append(_blk(w_down[0:22 * 128, m * 128:(m + 1) * 128], 22))
        parts.append(_blk(w_down[22 * 128:43 * 128, m * 128:(m + 1) * 128], 21))
    out = np.concatenate(parts, axis=1)
    assert out.shape == (128, 448512), out.shape
    return out


def _fm(v, nchunk):
    return np.ascontiguousarray(v.reshape(nchunk, 128).T)


def _consts():
    c = np.zeros((128, 1536), np.float32)
    c[:, 0:128] = np.eye(128, dtype=np.float32)
    c[:, 128:256] = 1.0
    jj, ii = np.meshgrid(np.arange(128), np.arange(128), indexing="ij")
    c[:, 256:384] = (ii >= jj).astype(np.float32)
    k = np.arange(128)[:, None]
    col = np.arange(256)[None, :]
    dist_idx = np.where(col < 128, col - k + 128, col - 128 - k)
    valid = (dist_idx >= 0) & (dist_idx <= 128)
    for g, d in enumerate(A_DIL):
        bk = _rel_bucket(np.clip(dist_idx, 0, 128) * d).astype(np.float32)
        c[:, 384 + 256 * g:384 + 256 * (g + 1)] = np.where(valid, bk, -1.0)
    for g, w in enumerate((2, 4, 8, 16)):
        t = np.arange(16)
        c[:, 1152 + 16 * g:1152 + 16 * (g + 1)] = (w / np.minimum(t + 1, w)).astype(np.float32)[None, :]
    r = np.arange(64)
    same_b = (r[:, None] // 4) == (r[None, :] // 4)
    causal = (r[:, None] % 4) <= (r[None, :] % 4)
    c[0:64, 1216:1280] = (same_b & causal).astype(np.float32)
    c[0:64, 1280:1344] = np.where(same_b & causal, 0.0, NEG)
    c[0:64, 1344:1408] = np.where(r[:, None] == r[None, :], 0.0, NEG)
    return c


_CACHE = {}


PROMPT_CORE_SEQ = {0: 0, 1: 1, 4: 2, 5: 3}
SEQ_CORE = [0, 1, 4, 5]


def _prepare(inp, cores=range(8)):
    f = lambda k: np.asarray(inp[k], dtype=np.float32)
    w_in, w_out, w_up, w_down, w_mkv = f("w_in"), f("w_out"), f("w_up"), f("w_down"), f("w_mem_kv")
    wst = np.stack([_pack_layer(w_in[l], w_out[l], w_up[l], w_down[l]) for l in range(2)])
    wmk = np.stack([np.concatenate([_blk(w_mkv[l][:, j * 256:(j + 1) * 256], 16) for j in range(6)], axis=1)
                    for l in range(2)])
    par = np.zeros((128, 2, NPAR), np.float32)
    for l in range(2):
        for off, key in ((P_PRE, "norm_pre_mix"), (P_MEM, "norm_mem"), (P_POSTM, "norm_post_mix"),
                         (P_PREF, "norm_pre_ffn"), (P_POSTF, "norm_post_ffn")):
            par[:, l, off:off + 16] = _fm(f(key)[l], 16)
        par[:, l, P_GV:P_GV + 6] = _fm(f("norm_v_b")[l], 6)
        par[:, l, P_CS:P_CS + 6] = _fm(f("pool_scale")[l], 6)
        for t in range(3):
            par[:, l, P_CW + 86 * t:P_CW + 86 * (t + 1)] = _fm(f("conv_w")[l, t], 86)
        par[:, l, P_CB:P_CB + 86] = _fm(f("conv_b")[l], 86)
    wsT = np.ascontiguousarray(f("w_spatial").transpose(0, 3, 1, 2))
    bsr = np.ascontiguousarray(f("b_spatial")[:, None, :, :])
    wcp = np.ascontiguousarray(f("w_pool").reshape(2, 6, 128, 192).transpose(0, 2, 1, 3))
    cst = _consts()
    ws_, bs__ = f("w_spatial"), f("b_spatial")
    wrep = np.ascontiguousarray(np.broadcast_to(ws_[:, :, 0:4, 0:4].transpose(3, 0, 1, 2)[None, :, :, :, None, :],
                                                (NB, 4, 2, 4, NB, 4))).reshape(64, 2, 4, 64)
    brep = np.ascontiguousarray(np.broadcast_to(bs__[:, :, None, 0:4], (2, 4, NB, 4))).reshape(1, 2, 4, 64)
    xp, xs, mp = f("x_prompt"), f("x_sample"), f("mem_prompt")
    ca = [f("cache_a_w128"), f("cache_a_w512"), f("cache_a_w2048")]
    cmem, spool, sconv = f("cache_mem_kv"), f("state_pool"), f("state_conv")

    def fmT(x):
        return np.ascontiguousarray(x.T.reshape(16, 128, x.shape[0]).transpose(1, 0, 2))

    in_maps = []
    for c in cores:
        bi = PROMPT_CORE_SEQ.get(c, -1)
        bs_ = slice(NB * c, NB * (c + 1))
        rows = [np.arange(128)] + [t + 4 * np.arange(128) for t in range(4)] + [t + 16 * np.arange(128) for t in range(4)]
        src = [0, 1, 1, 1, 1, 2, 2, 2, 2]
        caK = np.zeros((2, NB, 128, 9, 2, 128), np.float32)
        caV = np.zeros((2, NB, 128, 9, 2, 128), np.float32)
        for si in range(9):
            blk = ca[src[si]][:, bs_][:, :, rows[si]]
            caK[:, :, :, si] = blk[:, :, :, 0].transpose(0, 1, 4, 3, 2)
            caV[:, :, :, si] = blk[:, :, :, 1]
        cm = cmem[:, bs_].reshape(2, NB, 256, 2, 768)
        cmK = np.ascontiguousarray(cm[:, :, :, 0].reshape(2, NB, 256, 6, 128).transpose(0, 1, 4, 3, 2))
        cmV = np.ascontiguousarray(cm[:, :, :, 1].reshape(2, NB, 2, 128, 768).transpose(0, 1, 3, 2, 4))
        spT = np.ascontiguousarray(spool[:, bs_].reshape(2, NB, 15, 6, 128).transpose(0, 4, 3, 1, 2))
        scT = np.ascontiguousarray(sconv[:, bs_].reshape(2, NB, 2, 86, 128).transpose(0, 4, 3, 1, 2))
        in_maps.append({
            "xpT": fmT(xp[bi]) if bi >= 0 else np.zeros((128, KC, SEQ), np.float32),
            "xsT": fmT(xs[bs_].reshape(TS, D)),
            "mpT": fmT(mp[bi]) if bi >= 0 else np.zeros((128, KC, 256), np.float32),
            "wst": wst, "wmk": wmk, "par": par if bi >= 0 else np.zeros_like(par), "par2": par, "wsT": wsT, "bsr": bsr, "wcp": wcp,
            "relb": f("rel_bias"), "cst": cst,
            "caK": caK.reshape(2, NB, 128, 2304), "caV": caV.reshape(2, NB, 128, 2304),
            "cmK": cmK.reshape(2, NB, 128, 1536), "cmV": cmV.reshape(2, NB, 128, 1536),
            "spT": spT, "scT": scT, "wrep": wrep, "brep": brep,
        })
    return in_maps


def kernel(**inp):
    nc = build_program(_CACHE.get("stage", 99))
    in_maps = _prepare(inp)
    res = run_bass_kernel_spmd(nc, in_maps, core_ids=list(range(8)))
    return _assemble(res.results)


def _assemble(R):

    def tokmajor(a):
        return np.ascontiguousarray(a.transpose(2, 1, 0)).reshape(a.shape[2], -1)

    y_p = np.stack([tokmajor(R[SEQ_CORE[b]]["ypT"]) for b in range(4)])
    y_s = np.concatenate([tokmajor(R[c]["ysT"]).reshape(NB, 4, D) for c in range(8)], axis=0)
    a_p = [np.stack([R[SEQ_CORE[b]]["oa%d" % g] for b in range(4)], axis=1).reshape(2, 4, -1, 2, 2, 128) for g in range(3)]
    mkv_p = np.stack([R[SEQ_CORE[b]]["omkv"] for b in range(4)], axis=1).reshape(2, 4, 256, 2, 4, 192)
    pool_p = np.stack([R[SEQ_CORE[b]]["opool"].transpose(0, 3, 2, 1).reshape(2, 15, 768) for b in range(4)], axis=1)
    conv_p = np.stack([R[SEQ_CORE[b]]["oconv"].transpose(0, 3, 2, 1).reshape(2, 2, 2 * DFF) for b in range(4)], axis=1)
    a_s = [np.concatenate([R[c]["osa%d" % g].reshape(2, NB, 4, 2, 2, 128) for c in range(8)], axis=1) for g in range(3)]
    v_s = np.concatenate([R[c]["osv"].transpose(0, 3, 2, 1).reshape(2, NB, 4, 768) for c in range(8)], axis=1)
    pool_s = np.concatenate([R[c]["ospool"].transpose(0, 3, 4, 2, 1).reshape(2, NB, 15, 768) for c in range(8)], axis=1)
    conv_s = np.concatenate([R[c]["osconv"].transpose(0, 3, 4, 2, 1).reshape(2, NB, 2, 2 * DFF) for c in range(8)], axis=1)
    outs = (y_p, y_s, a_p[0], a_p[1], a_p[2], mkv_p, pool_p, conv_p, a_s[0], a_s[1], a_s[2], v_s, pool_s, conv_s)
    return tuple(np.ascontiguousarray(o, dtype=np.float32) for o in outs)
```
